# Optimizing a Trainium2 kernel written in Bass

```python
import math
import jax, jax.numpy as jnp
from jax import lax
import numpy as np

D_MODEL = 1024
BATCH = 2
SEQ = 8192
DEPTH = 2

MEM_LEN = 256
HEAD_DIM = 64
N_MIX_HEADS = D_MODEL // HEAD_DIM
N_MEM_HEADS = 4
N_TOK_HEADS = N_MIX_HEADS - N_MEM_HEADS
TOK_WIDTH = N_TOK_HEADS * HEAD_DIM
MEM_WIDTH = N_MEM_HEADS * HEAD_DIM
MIX_WIDTH = TOK_WIDTH + MEM_WIDTH
Q_LORA = 384
KV_LORA = 256
QK_NOPE = 64
QK_ROPE = 32
V_DIM = HEAD_DIM
QK_DIM = QK_NOPE + QK_ROPE
ROPE_THETA = 10000.0
Q_BLOCK = 128
CONV_W = 4
LRU_C = 8.0
N_LRU_BLOCKS = N_TOK_HEADS
LRU_BLOCK = TOK_WIDTH // N_LRU_BLOCKS
ALPHA = (2.0 * DEPTH) ** 0.25
BETA = (8.0 * DEPTH) ** -0.25
NORM_EPS = 1e-6
N_MLA = (DEPTH + 1) // 2
N_LRU = DEPTH // 2
MLA_IN = Q_LORA + KV_LORA + QK_ROPE + MIX_WIDTH + MEM_WIDTH
LRU_IN = TOK_WIDTH + MIX_WIDTH + MEM_WIDTH

kernel_name = "hybrid_mla_rglru_memory_deepnorm"


def _split(t, sizes):
    idx = np.cumsum(sizes)[:-1].tolist()
    return jnp.split(t, idx, axis=-1)


def rms_norm(t, g):
    t32 = t.astype(jnp.float32)
    t32 = t32 * lax.rsqrt(jnp.mean(t32 * t32, axis=-1, keepdims=True) + NORM_EPS)
    return (t32 * g.astype(jnp.float32)).astype(t.dtype)


def layer_norm(t, g, b):
    t32 = t.astype(jnp.float32)
    mu = jnp.mean(t32, axis=-1, keepdims=True)
    var = jnp.mean(jnp.square(t32 - mu), axis=-1, keepdims=True)
    y = (t32 - mu) * lax.rsqrt(var + NORM_EPS)
    return (y * g.astype(jnp.float32) + b.astype(jnp.float32)).astype(t.dtype)


def apply_rope(t, positions):
    half = t.shape[-1] // 2
    inv_freq = ROPE_THETA ** (-jnp.arange(half, dtype=jnp.float32) / half)
    ang = positions.astype(jnp.float32)[..., None] * inv_freq
    cos = jnp.cos(ang)[:, :, None, :].astype(t.dtype)
    sin = jnp.sin(ang)[:, :, None, :].astype(t.dtype)
    t1, t2 = t[..., :half], t[..., half:]
    return jnp.concatenate([t1 * cos - t2 * sin, t1 * sin + t2 * cos], axis=-1)


def causal_attention(q, k, v):
    b, s, h, d = q.shape
    nb = s // Q_BLOCK
    scale = 1.0 / math.sqrt(d)
    qb = q.reshape(b, nb, Q_BLOCK, h, d).transpose(1, 0, 2, 3, 4)
    k_pos = jnp.arange(s)

    def one_block(args):
        q_blk, blk = args
        sc = jnp.einsum('bqhd,bkhd->bhqk', q_blk, k,
                        preferred_element_type=jnp.float32) * scale
        q_pos = blk * Q_BLOCK + jnp.arange(Q_BLOCK)
        mask = k_pos[None, :] <= q_pos[:, None]
        sc = jnp.where(mask[None, None], sc, -jnp.inf)
        p = jax.nn.softmax(sc, axis=-1).astype(v.dtype)
        return jnp.einsum('bhqk,bkhd->bqhd', p, v)

    out = lax.map(one_block, (qb, jnp.arange(nb)))
    return out.transpose(1, 0, 2, 3, 4).reshape(b, s, h, v.shape[-1])


def memory_attention(q, mem_k, mem_v):
    sc = jnp.einsum('bshd,bmhd->bhsm', q, mem_k,
                    preferred_element_type=jnp.float32) / math.sqrt(HEAD_DIM)
    p = jax.nn.softmax(sc, axis=-1).astype(mem_v.dtype)
    return jnp.einsum('bhsm,bmhd->bshd', p, mem_v)


def _lin_rec_combine(left, right):
    a1, b1 = left
    a2, b2 = right
    return a1 * a2, a2 * b1 + b2


def rg_lru_branch(u, conv_w, conv_b, w_r, b_r, w_i, b_i, lam):
    b, s, w = u.shape
    u_pad = jnp.pad(u, ((0, 0), (CONV_W - 1, 0), (0, 0)))
    xc = conv_b + u_pad[:, 0:s] * conv_w[0]
    for tap in range(1, CONV_W):
        xc = xc + u_pad[:, tap:tap + s] * conv_w[tap]
    xb = xc.reshape(b, s, N_LRU_BLOCKS, LRU_BLOCK)
    r = jax.nn.sigmoid(jnp.einsum('bsgi,gij->bsgj', xb, w_r).reshape(b, s, w) + b_r)
    i = jax.nn.sigmoid(jnp.einsum('bsgi,gij->bsgj', xb, w_i).reshape(b, s, w) + b_i)
    log_a = (-LRU_C * jax.nn.softplus(-lam.astype(jnp.float32))) * r.astype(jnp.float32)
    a = jnp.exp(log_a)
    gated_x = jnp.sqrt(-jnp.expm1(2.0 * log_a)) * (i * xc).astype(jnp.float32)
    _, hs = lax.associative_scan(_lin_rec_combine, (a, gated_x), axis=1)
    return hs.astype(u.dtype)


def setup_inputs(seed: int = 0) -> dict:
    key = jax.random.key(seed)
    ks = jax.random.split(key, 24)
    f32 = jnp.float32
    nrm = lambda k, shape, s: jax.random.normal(k, shape, f32) * s
    x = nrm(ks[0], (BATCH, SEQ, D_MODEL), 1.0)
    mem = nrm(ks[1], (BATCH, MEM_LEN, D_MODEL), 1.0)
    offset = jax.random.randint(ks[2], (BATCH, 1), 0, 4096, dtype=jnp.int32)
    positions = (offset + jnp.arange(SEQ, dtype=jnp.int32)[None, :]).astype(jnp.int32)
    mla_w_in = nrm(ks[3], (N_MLA, D_MODEL, MLA_IN), D_MODEL ** -0.5)
    mla_q_norm = 1.0 + nrm(ks[4], (N_MLA, Q_LORA), 0.01)
    mla_w_uq = nrm(ks[5], (N_MLA, Q_LORA, N_TOK_HEADS * QK_DIM), Q_LORA ** -0.5)
    mla_kv_norm = 1.0 + nrm(ks[6], (N_MLA, KV_LORA), 0.01)
    mla_w_ukv = nrm(ks[7], (N_MLA, KV_LORA, N_TOK_HEADS * (QK_NOPE + V_DIM)), KV_LORA ** -0.5)
    lru_w_in = nrm(ks[8], (N_LRU, D_MODEL, LRU_IN), D_MODEL ** -0.5)
    lru_conv_w = nrm(ks[9], (N_LRU, CONV_W, TOK_WIDTH), CONV_W ** -0.5)
    lru_conv_b = nrm(ks[10], (N_LRU, TOK_WIDTH), 0.01)
    lru_w_rgate = nrm(ks[11], (N_LRU, N_LRU_BLOCKS, LRU_BLOCK, LRU_BLOCK), LRU_BLOCK ** -0.5)
    lru_b_rgate = nrm(ks[12], (N_LRU, TOK_WIDTH), 0.01)
    lru_w_igate = nrm(ks[13], (N_LRU, N_LRU_BLOCKS, LRU_BLOCK, LRU_BLOCK), LRU_BLOCK ** -0.5)
    lru_b_igate = nrm(ks[14], (N_LRU, TOK_WIDTH), 0.01)
    a_c = jax.random.uniform(ks[15], (N_LRU, TOK_WIDTH), f32, 0.9, 0.999)
    a0 = a_c ** (1.0 / LRU_C)
    lru_lambda = jnp.log(a0) - jnp.log1p(-a0)
    w_mem_kv = nrm(ks[16], (DEPTH, D_MODEL, 2 * MEM_WIDTH), D_MODEL ** -0.5)
    w_out = nrm(ks[17], (DEPTH, MIX_WIDTH, D_MODEL), BETA * MIX_WIDTH ** -0.5)
    ln_g = 1.0 + nrm(ks[18], (DEPTH, D_MODEL), 0.01)
    ln_b = nrm(ks[19], (DEPTH, D_MODEL), 0.01)
    return {"x": x, "mem": mem, "positions": positions,
            "mla_w_in": mla_w_in, "mla_q_norm": mla_q_norm, "mla_w_uq": mla_w_uq,
            "mla_kv_norm": mla_kv_norm, "mla_w_ukv": mla_w_ukv,
            "lru_w_in": lru_w_in, "lru_conv_w": lru_conv_w, "lru_conv_b": lru_conv_b,
            "lru_w_rgate": lru_w_rgate, "lru_b_rgate": lru_b_rgate,
            "lru_w_igate": lru_w_igate, "lru_b_igate": lru_b_igate, "lru_lambda": lru_lambda,
            "w_mem_kv": w_mem_kv, "w_out": w_out, "ln_g": ln_g, "ln_b": ln_b}


def reference(x, mem, positions, mla_w_in, mla_q_norm, mla_w_uq, mla_kv_norm, mla_w_ukv,
              lru_w_in, lru_conv_w, lru_conv_b, lru_w_rgate, lru_b_rgate,
              lru_w_igate, lru_b_igate, lru_lambda, w_mem_kv, w_out, ln_g, ln_b):
    b, s, _ = x.shape
    h = x
    for layer in range(DEPTH):
        j = layer // 2
        if layer % 2 == 0:
            z = h @ mla_w_in[j]
            c_q, c_kv, k_r, gate, q_mem = _split(
                z, [Q_LORA, KV_LORA, QK_ROPE, MIX_WIDTH, MEM_WIDTH])
            q = (rms_norm(c_q, mla_q_norm[j]) @ mla_w_uq[j]).reshape(b, s, N_TOK_HEADS, QK_DIM)
            q = jnp.concatenate([q[..., :QK_NOPE], apply_rope(q[..., QK_NOPE:], positions)], axis=-1)
            kv = (rms_norm(c_kv, mla_kv_norm[j]) @ mla_w_ukv[j]).reshape(
                b, s, N_TOK_HEADS, QK_NOPE + V_DIM)
            k_nope, v = kv[..., :QK_NOPE], kv[..., QK_NOPE:]
            k_rope = apply_rope(k_r[:, :, None, :], positions)
            k = jnp.concatenate(
                [k_nope, jnp.broadcast_to(k_rope, (b, s, N_TOK_HEADS, QK_ROPE))], axis=-1)
            tok = causal_attention(q, k, v).reshape(b, s, TOK_WIDTH)
        else:
            z = h @ lru_w_in[j]
            u, gate, q_mem = _split(z, [TOK_WIDTH, MIX_WIDTH, MEM_WIDTH])
            tok = rg_lru_branch(u, lru_conv_w[j], lru_conv_b[j], lru_w_rgate[j], lru_b_rgate[j],
                                lru_w_igate[j], lru_b_igate[j], lru_lambda[j])
        mem_kv = (mem @ w_mem_kv[layer]).reshape(b, MEM_LEN, 2, N_MEM_HEADS, HEAD_DIM)
        mem_out = memory_attention(q_mem.reshape(b, s, N_MEM_HEADS, HEAD_DIM),
                                   mem_kv[:, :, 0], mem_kv[:, :, 1]).reshape(b, s, MEM_WIDTH)
        y = jnp.concatenate([tok, mem_out], axis=-1) * jax.nn.silu(gate)
        h = layer_norm(ALPHA * h + y @ w_out[layer], ln_g[layer], ln_b[layer])
    return h
```

```python
import contextlib
import os
import math
import numpy as np
import concourse.bass as bass
import concourse.mybir as mybir
from concourse.bass_utils import run_bass_kernel_spmd

F32 = mybir.dt.float32
BF16 = mybir.dt.bfloat16
U8 = mybir.dt.uint8
I32 = mybir.dt.int32
AF = mybir.ActivationFunctionType
ALU = mybir.AluOpType

PHASE = 3000
COMPUTE = ("tensor", "vector", "scalar", "gpsimd")


class Op:
    __slots__ = ("eng", "fn", "deps", "idx", "is_dma", "id")


class Sched:
    def __init__(self, nc, ndma_sems=16):
        self.nc = nc
        self.ops = []
        self.per_eng = {e: [] for e in ("tensor", "vector", "scalar", "gpsimd", "sync")}
        self.last_w = {}
        self.readers = {}
        self.ndma_sems = ndma_sems
        self.dma_count = {}
        self.ccount = {}
        self.cc_ops = []
        self.barrier_dep = None

    def op(self, eng, fn, reads=(), writes=(), dma=False, extra_deps=(), merge=False):
        o = Op()
        o.eng = eng
        o.fn = fn
        o.is_dma = dma
        o.id = len(self.ops)
        deps = set(extra_deps)
        if self.barrier_dep is not None:
            deps.add(self.barrier_dep)
        for t in list(reads) + ([] if merge else list(writes)):
            for w in self.last_w.get(t, ()):
                deps.add(w)
        for t in writes:
            for r in self.readers.get(t, {}).values():
                deps.add(r)
        for t in reads:
            if t.startswith("bank"):
                for r in self.readers.get(t, {}).values():
                    deps.add(r)
        o.deps = sorted(deps)
        for t in writes:
            if merge:
                self.last_w[t] = set(self.last_w.get(t, ())) | {o.id}
            else:
                self.last_w[t] = {o.id}
                self.readers[t] = {}
        for t in reads:
            key = (eng,) if not dma else (eng, o.id)
            self.readers.setdefault(t, {})[key] = o.id
        if dma == "cc":
            o.idx = len(self.cc_ops)
            self.cc_ops.append(o)
        elif dma:
            n = self.dma_count.get(eng, 0)
            self.dma_count[eng] = n + 1
            o.idx = n
        else:
            o.idx = self.ccount.get(eng, 0)
            self.ccount[eng] = o.idx + 1
        self.per_eng[eng].append(o)
        self.ops.append(o)
        return o.id

    def barrier(self, mk, skip_cc=False, skip_dma=False):
        deps = set()
        for e, lst in self.per_eng.items():
            last_c = None
            for o in lst:
                if o.is_dma == "cc" and skip_cc:
                    continue
                if o.is_dma and skip_dma:
                    continue
                if o.is_dma:
                    deps.add(o.id)
                else:
                    last_c = o.id
            if last_c is not None:
                deps.add(last_c)
        b = self.op("gpsimd", mk, extra_deps=deps)
        self.barrier_dep = b
        return b

    def emit(self, final_wait_ops=()):
        nc = self.nc
        with contextlib.ExitStack() as st:
            csem = {}
            for e in COMPUTE:
                n = self.ccount.get(e, 0)
                nph = max(1, (n + PHASE - 1) // PHASE)
                csem[e] = [st.enter_context(nc.semaphore(f"c_{e}_{i}")) for i in range(nph)]
            dsem = {}
            for e, n in self.dma_count.items():
                k = min(self.ndma_sems, n)
                dsem[e] = [st.enter_context(nc.semaphore(f"d_{e}_{i}")) for i in range(k)]
            ccsem = [st.enter_context(nc.semaphore(f"cc_{i}")) for i in range(len(self.cc_ops))]
            block = st.enter_context(nc.Block())
            ops = self.ops

            def target(o):
                if o.is_dma == "cc":
                    return ccsem[o.idx], 1
                if o.is_dma:
                    k = len(dsem[o.eng])
                    return dsem[o.eng][o.idx % k], 16 * (o.idx // k + 1)
                return csem[o.eng][o.idx // PHASE], (o.idx % PHASE) + 1

            def run_engine(ename, eng):
                waited = {}
                waited_dma = set()
                for o in self.per_eng[ename]:
                    for d in o.deps:
                        p = ops[d]
                        if p.is_dma:
                            if d in waited_dma:
                                continue
                            waited_dma.add(d)
                            s, v = target(p)
                            eng.wait_ge(s, v)
                        else:
                            if p.eng == ename and ename == "tensor":
                                continue
                            if waited.get(p.eng, -1) >= p.idx:
                                continue
                            waited[p.eng] = p.idx
                            s, v = target(p)
                            eng.wait_ge(s, v)
                    if o.is_dma == "cc":
                        s, v = target(o)
                        o.fn(eng).then_inc(s)
                    elif o.is_dma:
                        k = len(dsem[o.eng])
                        if o.idx >= k:
                            eng.wait_ge(dsem[o.eng][o.idx % k], 16 * (o.idx // k))
                        s, v = target(o)
                        o.fn(eng).then_inc(s, 16)
                    else:
                        s, v = target(o)
                        o.fn(eng).then_inc(s, 1)
                if ename == "sync":
                    for fo in final_wait_ops:
                        s, v = target(ops[fo])
                        eng.wait_ge(s, v)

            @block.sync
            def _(e):
                run_engine("sync", e)

            @block.tensor
            def _(e):
                run_engine("tensor", e)

            @block.vector
            def _(e):
                run_engine("vector", e)

            @block.scalar
            def _(e):
                run_engine("scalar", e)

            @block.gpsimd
            def _(e):
                run_engine("gpsimd", e)


NT = 2048
NB = 4
TT = 16
SEQ = 8192
ALPHA = 4.0 ** 0.25
EPS = 1e-6
SC_ATT = 1.0 / math.sqrt(96.0)
SC_MEM = 1.0 / 8.0
TWO_PI = float(2.0 * np.pi)
DEBUG = False
NHEADS_DBG = int(os.environ.get("KHEADS", "12"))
KF = int(os.environ.get("KF", "9"))


class _Stop(Exception):
    pass


STAGE = int(os.environ.get("KSTAGE", "99"))


def build():
    nc = bass.Bass("TRN2", target_bir_lowering=False)
    DUMPS = {}

    def ckpt(n, dumps=None):
        if STAGE == n:
            DUMPS.update(dumps() if dumps else {})
            raise _Stop()

    def din(name, shape, dt=F32):
        return nc.dram_tensor(name, list(shape), dt, kind="ExternalInput").ap()

    x = din("x", [NT, 1024])
    posb = din("posb", [32, NT], I32)
    cpack = din("cpack", [128, 64])
    l1pack = din("l1pack", [128, 48])
    mem = din("mem", [256, 1024])
    masks = din("masks", [128, 16, 512], U8)
    w_in0 = din("w_in0", [1024, 1952])
    w_krs = din("w_krs", [1024, 32])
    w_uq = din("w_uq", [384, 1152])
    w_uqs = din("w_uqs", [384, 1152])
    w_ukv = din("w_ukv", [256, 1536])
    w_in1 = din("w_in1", [1024, 2048])
    w_rbd = din("w_rbd", [128, 6, 128])
    w_ibd = din("w_ibd", [128, 6, 128])
    w_mem = din("w_mem", [2, 1024, 512])
    w_out = din("w_out", [2, 1024, 1024])
    ln_g = din("ln_g", [2, 1024])
    ln_b = din("ln_b", [2, 1024])
    out = nc.dram_tensor("out", [NT, 1024], F32, kind="ExternalOutput").ap()
    dbgout = nc.dram_tensor("dbgout", [128, 8192], F32, kind="ExternalOutput").ap() if STAGE != 99 else None

    lat_b = [nc.dram_tensor(f"lat_b{i}", [n, NT], BF16) for i, n in enumerate((128, 128, 32))]
    lat_g = [nc.dram_tensor(f"lat_g{i}", [4 * n, NT], BF16) for i, n in enumerate((128, 128, 32))]
    hs = nc.dram_tensor("hs", [NT, 1024], F32)
    uh_b = nc.dram_tensor("uh_b", [768, 16], F32)
    uh_g = nc.dram_tensor("uh_g", [4 * 768, 16], F32)
    sm_b = nc.dram_tensor("sm_b", [768, 8], F32)
    sm_g = nc.dram_tensor("sm_g", [4 * 768, 8], F32)
    groups = [[0, 1, 2, 3], [4, 5, 6, 7]]

    with contextlib.ExitStack() as st:
        def sb(name, shape, dt):
            return st.enter_context(nc.sbuf_tensor(name, list(shape), dt))

        banks = [st.enter_context(nc.psum_tensor(f"pb{i}", [128, 512], F32)) for i in range(8)]
        S = Sched(nc)

        A1 = sb("A1", [128, 32 * 1024], U8)
        A2 = sb("A2", [128, 24 * 1024], U8)
        A3 = sb("A3", [128, 48 * 1024], U8)
        A4 = sb("A4", [128, 20 * 1024], U8)

        def view(arena, off, nbytes, dt, pat=None, **kw):
            v = arena[:, off:off + nbytes].bitcast(dt)
            if pat:
                v = v.rearrange(pat, **kw)
            return v

        XT = view(A1, 0, 32768, BF16, "p (k t) -> p k t", k=8)
        LAT = view(A1, 0, 32768, BF16, "p (k t) -> p k t", k=2)
        WIN = view(A3, 0, 32768, BF16, "p (k c) -> p k c", k=8)
        WG1 = view(A3, 24576, 20480, BF16, "p (k c) -> p k c", k=8)
        CTAB = view(A2, 0, 8192, F32)
        STAB = view(A2, 8192, 8192, F32)
        PI32 = view(A4, 0, 8192, I32)
        CT2 = view(A4, 8192, 8192, F32)
        KRo = view(A2, 16384, 4096, BF16)
        WMEM = view(A3, 32768, 8192, BF16, "p (k c) -> p k c", k=8)
        MT = view(A3, 40960, 4096, BF16, "p (k c) -> p k c", k=8)
        TB = [view(A3, 45056 + i * 2048, 2048, BF16) for i in range(2)]
        RB = [view(A3, i * 4096, 4096, F32) for i in range(6)]
        LG = view(A3, 32768, 4096, F32)
        LB = view(A3, 36864, 4096, F32)
        UX = [view(A4, i * 2064, 2064, F32) for i in range(2)]
        XCB = [view(A4, 4128 + i * 1024, 1024, BF16) for i in range(2)]
        WR = view(A4, 6176, 1536, BF16, "p (k c) -> p k c", k=6)
        WI = view(A4, 7712, 1536, BF16, "p (k c) -> p k c", k=6)
        KT = [view(A2, 0, 16384, BF16)] * 2
        HL = view(A2, 0, 12288 * 2, BF16, "p (k t) -> p k t", k=6)
        PP = view(A3, 0, 12288 * 2, BF16, "p (k t) -> p k t", k=6)
        QT = view(A3, 0, 49152, BF16, "p (h t) -> p h t", h=12)
        CQT = view(A4, 0, 12288, BF16, "p (k t) -> p k t", k=3)
        LATo = view(A4, 12288, 8192, BF16, "p (k t) -> p k t", k=2)
        VA = [view(A4, i * 8448, 64 * 66 * 2, BF16, "p (t c) -> p t c", c=66) for i in range(2)]

        G = sb("G", [128, 8, NT], BF16)
        QMB = [sb("QMB0", [128, 2, 512], BF16)] * 2
        WSM = sb("WSM", [128, 8, 1024], BF16)
        WKS = sb("WKS", [128, 8, 96], BF16)
        MSK = sb("MSK", [128, 16, 512], U8)
        PT = [sb(f"PT{i}", [128, 512], BF16) for i in range(4)]
        ident = sb("ident", [128, 128], BF16)
        identf = sb("identf", [128, 128], F32)
        ones_bf = sb("ones_bf", [128, 128], BF16)
        ones_f = sb("ones_f", [128, 128], F32)
        smallc = sb("smallc", [128, 64], F32)
        MK = sb("MK", [128, 2, 256], BF16)
        MVA = sb("MVA", [128, 4, 2, 66], BF16)
        TMP = [sb(f"TMP{i}", [128, 512], F32) for i in range(6)]
        RD = TMP[4]
        stat = sb("stat", [128, 32], F32)
        STATS = [stat[:, 0:17], smallc[:, 16:33], smallc[:, 33:50]]
        L1C = sb("L1C", [128, 64], F32)
        HALO = sb("HALO", [128, 6, 16], F32)
        SUMS = sb("SUMS", [128, 6, 8], F32)
        dbg = None

        bank_rr = [0]

        def nb(lst=(0, 1, 2, 3, 4, 5, 6, 7)):
            i = lst[bank_rr[0] % len(lst)]
            bank_rr[0] += 1
            return banks[i], f"bank{i}"

        def mm(out_ap, pairs, reads, writes):
            def fn(e):
                n = len(pairs)
                ins = None
                for i, (l, r) in enumerate(pairs):
                    ins = e.matmul(out_ap, lhsT=l, rhs=r, start=(i == 0), stop=(i == n - 1))
                return ins
            return S.op("tensor", fn, reads=reads, writes=writes)

        def dma(eng, out_ap, in_ap, reads=(), writes=(), extra_deps=(), merge=False, **kw):
            return S.op(eng, lambda e: e.dma_start(out=out_ap, in_=in_ap, **kw), reads=reads, writes=writes, dma=True,
                        extra_deps=extra_deps, merge=merge)

        def act(out_ap, in_ap, func, reads, writes, **kw):
            return S.op("scalar", lambda e: e.activation(out=out_ap, in_=in_ap, func=func, **kw), reads=reads, writes=writes)

        def vcopy(eng, out_ap, in_ap, reads, writes):
            return S.op(eng, lambda e: e.tensor_copy(out=out_ap, in_=in_ap), reads=reads, writes=writes)

        def tt(eng, out_ap, a, b, op, reads, writes):
            return S.op(eng, lambda e: e.tensor_tensor(out=out_ap, in0=a, in1=b, op=op), reads=reads, writes=writes)

        def ts(eng, out_ap, a, s1, s2, op0, op1, reads, writes):
            if op1 is None:
                return S.op(eng, lambda e: e.tensor_scalar(out=out_ap, in0=a, scalar1=s1, scalar2=None, op0=op0), reads=reads, writes=writes)
            return S.op(eng, lambda e: e.tensor_scalar(out=out_ap, in0=a, scalar1=s1, scalar2=s2, op0=op0, op1=op1), reads=reads, writes=writes)

        def stt(out_ap, a, s, b, op0, op1, reads, writes):
            return S.op("vector", lambda e: e.scalar_tensor_tensor(out=out_ap, in0=a, scalar=s, in1=b, op0=op0, op1=op1), reads=reads, writes=writes)

        def blk(c):
            return slice(c * 512, (c + 1) * 512)

        def wload(dst3, src2, writes):
            K, C = dst3.shape[1], dst3.shape[2]
            first = None
            for k in range(K):
                for c0 in range(0, C, 1024):
                    c1 = min(C, c0 + 1024)
                    if first is None:
                        first = dma("gpsimd", dst3[:, k, c0:c1], src2[k * 128:(k + 1) * 128, c0:c1], writes=writes)
                    else:
                        dma("gpsimd", dst3[:, k, c0:c1], src2[k * 128:(k + 1) * 128, c0:c1], writes=writes,
                            extra_deps=S.ops[first].deps, merge=True)

        S.op("gpsimd", lambda e: e.memset(identf[:], 1.0), writes=["identf"])
        S.op("gpsimd", lambda e: e.affine_select(out=identf[:], in_=identf[:], pattern=[[-1, 128]], compare_op=ALU.is_equal, fill=0.0, base=0, channel_multiplier=1), writes=["identf"])
        vcopy("vector", ident[:], identf[:], ["identf"], ["ident"])
        S.op("gpsimd", lambda e: e.memset(ones_bf[:], 1.0), writes=["ones_bf"])
        S.op("gpsimd", lambda e: e.memset(ones_f[:], 1.0), writes=["ones_f"])
        S.op("gpsimd", lambda e: e.memset(WKS[:], 0.0), writes=["WKS"])
        S.op("gpsimd", lambda e: e.memset(MVA[:, :, :, 64:66], 1.0), writes=["MVAones"])
        dma("sync", MSK[:], masks[:, :, :], writes=["MSK"])
        dma("sync", smallc[:], cpack[:, :], writes=["qn", "kvn", "invf", "sgn", "SEL"])
        SEL = smallc[:, 8:12]

        dma("sync", PI32[64:96, :], posb[:, :], writes=["PI32"])
        R = slice(64, 96)
        vcopy("vector", CTAB[R, :], PI32[R, :], ["PI32"], ["CTAB"])
        ts("vector", CTAB[R, :], CTAB[R, :], smallc[R, 5:6], None, ALU.mult, None, ["CTAB", "invf"], ["CTAB"])

        H = STAB
        H2 = CT2
        ts("vector", H[R, :], CTAB[R, :], 1.0 / TWO_PI, None, ALU.mult, None, ["CTAB"], ["STAB"])
        vcopy("vector", PI32[R, :], H[R, :], ["STAB"], ["PI32"])
        vcopy("vector", H[R, :], PI32[R, :], ["PI32"], ["STAB"])
        stt(H[R, :], H[R, :], -TWO_PI, CTAB[R, :], ALU.mult, ALU.add, ["STAB", "CTAB"], ["STAB"])
        ts("vector", H[R, :], H[R, :], 0.5, None, ALU.mult, None, ["STAB"], ["STAB"])
        tt("vector", H2[R, :], H[R, :], H[R, :], ALU.mult, ["STAB"], ["CT2"])
        SP = PI32[:].bitcast(F32)
        sc = [-1.0 / 39916800.0, 1.0 / 362880.0, -1.0 / 5040.0, 1.0 / 120.0, -1.0 / 6.0, 1.0]
        ts("vector", SP[R, :], H2[R, :], sc[0], None, ALU.mult, None, ["CT2", "PI32"], ["PI32"])
        for cf in sc[1:5]:
            stt(SP[R, :], SP[R, :], cf, H2[R, :], ALU.add, ALU.mult, ["PI32", "CT2"], ["PI32"])
        stt(SP[R, :], SP[R, :], sc[5], H[R, :], ALU.add, ALU.mult, ["PI32", "STAB"], ["PI32"])
        cc = [1.0 / 479001600.0, -1.0 / 3628800.0, 1.0 / 40320.0, -1.0 / 720.0, 1.0 / 24.0, -0.5]
        ts("vector", CTAB[R, :], H2[R, :], cc[0], None, ALU.mult, None, ["CT2", "STAB", "CTAB"], ["CTAB"])
        for cf in cc[1:6]:
            stt(CTAB[R, :], CTAB[R, :], cf, H2[R, :], ALU.add, ALU.mult, ["CTAB", "CT2"], ["CTAB"])
        ts("vector", CTAB[R, :], CTAB[R, :], 1.0, None, ALU.add, None, ["CTAB"], ["CTAB"])
        stt(STAB[R, :], SP[R, :], 2.0, CTAB[R, :], ALU.mult, ALU.mult, ["PI32", "CTAB", "STAB"], ["STAB"])
        tt("vector", CTAB[R, :], SP[R, :], SP[R, :], ALU.mult, ["PI32", "CTAB", "STAB"], ["CTAB"])
        ts("vector", CTAB[R, :], CTAB[R, :], -2.0, 1.0, ALU.mult, ALU.add, ["CTAB"], ["CTAB"])
        ts("vector", STAB[R, :], STAB[R, :], smallc[R, 6:7], None, ALU.mult, None, ["STAB", "sgn"], ["STAB"])

        def make_xt(src_dram):
            for t in range(TT):
                tb = TB[t % 2]
                dma("gpsimd", tb[:], src_dram[t * 128:(t + 1) * 128, :], writes=[f"TB{t % 2}"])
                pb, pbt = nb()
                pv = pb[:].bitcast(BF16).rearrange("p (k t) -> p k t", k=8)

                def fn(e, tb=tb, pv=pv):
                    ins = None
                    for k in range(8):
                        ins = e.transpose(out=pv[:, k, :], in_=tb[:, k * 128:(k + 1) * 128], identity=ident[:])
                    return ins
                S.op("tensor", fn, reads=[f"TB{t % 2}", "ident"], writes=[pbt])
                eng = "vector" if t % 2 == 0 else "scalar"
                if eng == "vector":
                    vcopy("vector", XT[:, :, t * 128:(t + 1) * 128], pv, [pbt], [f"XT{t // 4}"])
                else:
                    S.op("scalar", lambda e, pv=pv, t=t: e.copy(out=XT[:, :, t * 128:(t + 1) * 128], in_=pv), reads=[pbt], writes=[f"XT{t // 4}"])

        final_ops = []
        try:
            make_xt(x)
            ckpt(1, lambda: {"XT": (XT[:, :, 0:512], slice(0, 128), 4096, 0), "CTAB": (CTAB[64:96, 0:512], slice(64, 96), 512, 4096), "STAB": (STAB[64:96, 0:512], slice(64, 96), 512, 4608)})

            def mem_kv(layer):
                wload(WMEM[:], w_mem[layer], ["WMEM"])
                for mt in range(2):
                    tb = TB[0]
                    dma("gpsimd", tb[:], mem[mt * 128:(mt + 1) * 128, :], writes=["TB0"])
                    pb, pbt = nb()
                    pv = pb[:].bitcast(BF16).rearrange("p (k t) -> p k t", k=8)

                    def fn(e, tb=tb, pv=pv):
                        ins = None
                        for k in range(8):
                            ins = e.transpose(out=pv[:, k, :], in_=tb[:, k * 128:(k + 1) * 128], identity=ident[:])
                        return ins
                    S.op("tensor", fn, reads=["TB0", "ident"], writes=[pbt])
                    vcopy("vector", MT[:, :, mt * 128:(mt + 1) * 128], pv, [pbt], ["MT"])
                for ch in range(2):
                    pb, pbt = nb()
                    mm(pb[:, 0:256], [(WMEM[:, k, ch * 128:(ch + 1) * 128], MT[:, k, :]) for k in range(8)], ["WMEM", "MT"], [pbt])
                    vcopy("vector", MK[:, ch, :], pb[:, 0:256], [pbt], ["MK"])
                for mt in range(2):
                    pb, pbt = nb()
                    mm(pb[:, 0:256], [(MT[:, k, mt * 128:(mt + 1) * 128], WMEM[:, k, 256:512]) for k in range(8)], ["WMEM", "MT"], [pbt])
                    vcopy("vector", MVA[:, :, mt, 0:64], pb[:, 0:256].rearrange("p (h d) -> p h d", h=4), [pbt, "MVAones"], ["MVA"])

            def proj_chunk(c, wcols, k_n=8, w=None, rhs=None, rtok=None, wtok="WIN"):
                pb, pbt = nb()
                w = WIN if w is None else w
                rhs = XT if rhs is None else rhs
                rtok = f"XT{c}" if rtok is None else rtok
                mm(pb[0:(wcols.stop - wcols.start), :], [(w[:, k, wcols], rhs[:, k, blk(c)]) for k in range(k_n)], [wtok, rtok], [pbt])
                return pb, pbt

            def gate_qmem(c, col0, w=None, wtok="WIN", wtoks=None):
                for m in range(8):
                    pb, pbt = proj_chunk(c, slice(col0 + m * 128, col0 + (m + 1) * 128), w=w, wtok=(wtoks[m] if wtoks else wtok))
                    act(G[:, m, blk(c)], pb[:], AF.Silu, [pbt], [f"G{m}_{c}"])
                for m in range(2):
                    pb, pbt = proj_chunk(c, slice(col0 + 1024 + m * 128, col0 + 1024 + (m + 1) * 128), w=w, wtok=(wtoks[8 + m] if wtoks else wtok))
                    vcopy("vector", QMB[c % 2][:, m, :], pb[:], [pbt], ["QMB0"])
                ckpt(31, lambda: {"CQT": (CQT[:, :, 0:512], slice(0, 128), 1536, 0), "LATo": (LATo[:, :, 0:512], slice(0, 128), 1024, 1536),
                         "KRo": (KRo[64:96, 0:512], slice(64, 96), 512, 2560), "G": (G[:, :, 0:512], slice(0, 128), 4096, 3072)})
                mem_attention(c)
                ckpt(32, lambda: {"G": (G[:, :, 0:512], slice(0, 128), 4096, 3072)})

            att_state = {"i": 0}
            PTX = list(PT) + [TMP[5][:, 0:256].bitcast(BF16), TMP[5][:, 256:512].bitcast(BF16)]

            def attend(tiles, q_ap, q_tok, scale, o_bank, o_tok, npart, finalize, sb_list=(0, 1, 2), npt=4, LA=2, defer=None):
                n = len(tiles)
                pend = []
                nsb = len(sb_list)

                def issue_qk(i):
                    kT, ktok, v, vtok, mk = tiles[i]
                    gi = att_state["i"]
                    att_state["i"] += 1
                    pb = banks[sb_list[gi % nsb]]
                    pbt = f"bank{sb_list[gi % nsb]}"
                    mm(pb[:], [(kT, q_ap)], (list(ktok) if isinstance(ktok, list) else [ktok]) + [q_tok], [pbt])
                    pt = PTX[gi % npt]
                    ptt = f"PT{gi % npt}"
                    act(pt[:], pb[:], AF.Exp, [pbt], [ptt], scale=scale)
                    if mk is not None:
                        tt("vector", pt[:], pt[:], mk, ALU.mult, [ptt, "MSK"], [ptt])
                    pend.append((i, pt, ptt, v, vtok))

                def issue_pv():
                    i, pt, ptt, v, vtok = pend.pop(0)
                    S.op("tensor", lambda e: e.matmul(o_bank[0:66, :], lhsT=v, rhs=pt[:], start=(i == 0), stop=(i == n - 1)),
                         reads=[ptt, vtok], writes=[o_tok])

                for i in range(n):
                    issue_qk(i)
                    if i >= LA:
                        issue_pv()
                    if defer is not None and i == 8 and defer:
                        defer.pop(0)()
                while pend:
                    issue_pv()
                if defer is None:
                    finalize()
                else:
                    defer.append(finalize)

            fin_rr = [0]

            def finalize_head(o_bank, o_tok, chunk, par, c, bc=(5, 6), rd_alt=False):
                rows = slice(par * 64, par * 64 + 64)
                gt = f"G{chunk}_{c}"
                rdi = 4 + (fin_rr[0] % 2 if rd_alt else 0)
                RDx, rdt = TMP[rdi], f"TMP{rdi}"
                S.op("vector", lambda e: e.reciprocal(out=RDx[64:65, :], in_=o_bank[64:65, :]), reads=[o_tok], writes=[rdt])
                pb, pbt = nb(bc)
                S.op("tensor", lambda e: e.matmul(pb[:], lhsT=ones_f[64:65, :], rhs=RDx[64:65, :], start=True, stop=True), reads=[rdt, "ones_f"], writes=[pbt])
                i = fin_rr[0] % 2
                fin_rr[0] += 1
                t1, t1t = TMP[i], f"TMP{i}"
                t2, t2t = TMP[2 + i], f"TMP{2 + i}"
                S.op("scalar", lambda e: e.copy(out=t1[rows, :], in_=o_bank[0:64, :]), reads=[o_tok], writes=[t1t])
                tt("vector", t2[rows, :], pb[rows, :], G[rows, chunk, blk(c)], ALU.mult, [pbt, gt], [t2t])
                tt("gpsimd", G[rows, chunk, blk(c)], t1[rows, :], t2[rows, :], ALU.mult, [t1t, t2t], [gt])

            def mem_attention(c):
                if True:
                    fins = []
                    for m in range(4):
                        par = m % 2
                        rows = slice(par * 64, par * 64 + 64)
                        ob = banks[(3, 4, 7, 6)[m]]
                        obt = f"bank{(3, 4, 7, 6)[m]}"
                        tiles = [(MK[rows, m // 2, mt * 128:(mt + 1) * 128], "MK", MVA[:, m, mt, :], "MVA", None) for mt in range(2)]
                        attend(tiles, QMB[c % 2][rows, m // 2, :], "QMB0", SC_MEM, ob, obt, 64,
                               lambda ob=ob, obt=obt, m=m, par=par, c=c: finalize_head(ob, obt, 6 + m // 2, par, c, bc=(5, 0), rd_alt=True),
                               defer=fins)
                    while fins:
                        fins.pop(0)()

            def out_ln(layer, res_src, dst_dram, also_xt, w_prefetched=False):
                if not w_prefetched:
                    wload(WSM[:], w_out[layer], ["WSM", "WSMs"])
                dma("sync", LG[:], ln_g[layer:layer + 1, :].partition_broadcast(128), writes=["LG"])
                dma("sync", LB[:], ln_b[layer:layer + 1, :].partition_broadcast(128), writes=["LB"])
                def load_res(t):
                    dma("sync", RB[t % 6][:], res_src[t * 128:(t + 1) * 128, :], writes=[f"RB{t % 6}"])
                for t in range(6):
                    load_res(t)
                def s1(t):
                    rb, rbt = RB[t % 6], f"RB{t % 6}"
                    stat, stt_ = STATS[t % 3], f"stat{t % 3}"
                    c = t // 4
                    for n_ in range(2):
                        pb, pbt = nb()
                        mm(pb[:], [(G[:, k, t * 128:(t + 1) * 128], WSM[:, k, n_ * 512:(n_ + 1) * 512]) for k in range(8)],
                           [f"G{m}_{c}" for m in range(8)] + ["WSM"], [pbt])
                        stt(rb[:, n_ * 512:(n_ + 1) * 512], rb[:, n_ * 512:(n_ + 1) * 512], ALPHA, pb[:], ALU.mult, ALU.add, [pbt, rbt], [rbt])
                        S.op("vector", lambda e, rb=rb, n_=n_, stat=stat: e.bn_stats(out=stat[:, n_ * 6:(n_ + 1) * 6], in_=rb[:, n_ * 512:(n_ + 1) * 512]), reads=[rbt], writes=[stt_])
                    S.op("vector", lambda e, stat=stat: e.bn_aggr(out=stat[:, 12:14], in_=stat[:, 0:12]), reads=[stt_], writes=[stt_])
                    ts("vector", stat[:, 14:15], stat[:, 13:14], EPS, None, ALU.add, None, [stt_], [stt_])
                    act(stat[:, 15:16], stat[:, 14:15], AF.Ln, [stt_], [stt_])
                    act(stat[:, 16:17], stat[:, 15:16], AF.Exp, [stt_], [stt_], scale=-0.5)

                def s3(t):
                    rb, rbt = RB[t % 6], f"RB{t % 6}"
                    stat, stt_ = STATS[t % 3], f"stat{t % 3}"
                    ts("vector", rb[:], rb[:], stat[:, 12:13], stat[:, 16:17], ALU.subtract, ALU.mult, [rbt, stt_], [rbt])
                    tt("gpsimd", rb[:], rb[:], LG[:], ALU.mult, [rbt, "LG"], [rbt])
                    tt("gpsimd", rb[:], rb[:], LB[:], ALU.add, [rbt, "LB"], [rbt])
                    o = dma("sync", dst_dram[t * 128:(t + 1) * 128, :], rb[:], reads=[rbt], writes=[f"dst{t}"])
                    final_ops.append(o)
                    if also_xt:
                        tb = TB[t % 2]
                        tbt = f"TB{t % 2}"
                        S.op("scalar", lambda e, tb=tb, rb=rb: e.copy(out=tb[:], in_=rb[:]), reads=[rbt], writes=[tbt])
                    if t + 6 < TT:
                        load_res(t + 6)

                def s3b(t):
                    if also_xt:
                        tb = TB[t % 2]
                        tbt = f"TB{t % 2}"
                        pb, pbt = nb()
                        pv = pb[:].bitcast(BF16).rearrange("p (k t) -> p k t", k=8)

                        def fn(e, tb=tb, pv=pv):
                            ins = None
                            for k in range(8):
                                ins = e.transpose(out=pv[:, k, :], in_=tb[:, k * 128:(k + 1) * 128], identity=ident[:])
                            return ins
                        S.op("tensor", fn, reads=[tbt, "ident"], writes=[pbt])
                        S.op("scalar", lambda e, pv=pv, t=t: e.copy(out=XT[:, :, t * 128:(t + 1) * 128], in_=pv), reads=[pbt], writes=[f"XT{t // 4}"])

                for t in range(TT + 3):
                    if t < TT:
                        s1(t)
                    if 0 <= t - 2 < TT:
                        s3(t - 2)
                    if 0 <= t - 3 < TT:
                        s3b(t - 3)


            wload(WIN[:, :, 0:672], w_in0[:, 0:672], ["WIN"])
            wload(WIN[:, :, 672:1312], w_in0[:, 672:1312], ["WINb"])
            wload(WIN[:, :, 1312:1952], w_in0[:, 1312:1952], ["WINc"])
            dma("gpsimd", WKS[:, :, 64:96], w_krs.rearrange("(k p) c -> p k c", p=128), writes=["WKS"])
            WFLAT = WSM[:].rearrange("p k c -> p (k c)")
            WUQ = WFLAT[:, 0:3456].rearrange("p (k c) -> p k c", k=3)
            WUQS = WFLAT[:, 3456:6912].rearrange("p (k c) -> p k c", k=3)
            wload(WUQ, w_uq, ["WSM"])
            wload(WUQS, w_uqs, ["WSMs"])
            mem_kv(0)
            ckpt(2, lambda: {"MK": (MK[:], slice(0, 128), 512, 0), "MVA": (MVA[:, :, 0, :], slice(0, 128), 264, 512)})
            S.barrier(lambda e: e.memset(stat[:, 19:20], 0.0), skip_dma=True)
            for c in range(NB):
                for (nch, col0, ncols, gcol, dst, dtok) in ((3, 0, 384, 0, CQT, "CQT"), (2, 384, 256, 3, LATo, "LATo")):
                    c32 = []
                    sqs = []
                    for i in range(nch):
                        pb, pbt = proj_chunk(c, slice(col0 + i * 128, col0 + (i + 1) * 128))
                        sq = PT[i]
                        if KF >= 2:
                            act(sq[:], pb[:], AF.Square, [pbt], [f"PT{i}"])
                        vcopy("vector", TMP[2 + i][:], pb[:], [pbt], [f"TMP{2 + i}"])
                        sqs.append((sq, f"PT{i}"))
                        c32.append((TMP[2 + i], f"TMP{2 + i}"))
                    pb, pbt = nb()
                    if KF >= 3:
                        mm(pb[:], [(ones_bf[:], sq[:]) for sq, _ in sqs], [t for _, t in sqs] + ["ones_bf"], [pbt])
                        ts("vector", TMP[5][:], pb[:], 1.0 / ncols, EPS, ALU.mult, ALU.add, [pbt], ["TMP5"])
                    if KF >= 4:
                        act(TMP[5][:], TMP[5][:], AF.Ln, ["TMP5"], ["TMP5"])
                        act(TMP[5][:], TMP[5][:], AF.Exp, ["TMP5"], ["TMP5"], scale=-0.5)
                    for i in range(nch if KF >= 5 else 0):
                        t32, t32t = c32[i]
                        stt(dst[:, i, blk(c)], t32[:], smallc[:, gcol + i:gcol + i + 1], TMP[5][:], ALU.mult, ALU.mult,
                            [t32t, "TMP5", "qn", "kvn"], [f"{dtok}{c}"])
                    ckpt(311 if dtok == "CQT" else 312, lambda: {"CQT": (CQT[:, :, 0:512], slice(0, 128), 1536, 0), "LATo": (LATo[:, :, 0:512], slice(0, 128), 1024, 1536)})
                pb, pbt = proj_chunk(c, slice(576, 672))
                pb2, pbt2 = proj_chunk(c, slice(0, 96), w=WKS, wtok="WKS")
                tt("vector", TMP[0][R, :], pb[R, :], CTAB[R, blk(c)], ALU.mult, [pbt, "CTAB"], ["TMP0"])
                tt("vector", TMP[1][R, :], pb2[R, :], STAB[R, blk(c)], ALU.mult, [pbt2, "STAB"], ["TMP1"])
                tt("vector", KRo[R, blk(c)], TMP[0][R, :], TMP[1][R, :], ALU.add, ["TMP0", "TMP1"], ["KRo"])
                ckpt(313, lambda: {"CQT": (CQT[:, :, 0:512], slice(0, 128), 1536, 0), "LATo": (LATo[:, :, 0:512], slice(0, 128), 1024, 1536), "KRo": (KRo[64:96, 0:512], slice(64, 96), 512, 2560)})
                gate_qmem(c, 672, wtoks=["WINb"] * 5 + ["WINc"] * 5)

            ckpt(3, lambda: {"CQT": (CQT[:, :, 0:512], slice(0, 128), 1536, 0), "LATo": (LATo[:, :, 0:512], slice(0, 128), 1024, 1536),
                     "KRo": (KRo[64:96, 0:512], slice(64, 96), 512, 2560), "G": (G[:, :, 0:512], slice(0, 128), 4096, 3072)})
            for k in range(2):
                dma("sync", lat_b[k].ap(), LATo[:, k, :], reads=[f"LATo{c}" for c in range(NB)], writes=[f"lat_b{k}"])
            dma("sync", lat_b[2].ap(), KRo[R, :], reads=["KRo"], writes=["lat_b2"])
            for i in range(3):
                S.op("gpsimd", lambda e, i=i: e.collective_compute("AllGather", ALU.bypass, replica_groups=groups, ins=[lat_b[i].ap().opt()], outs=[lat_g[i].ap().opt()]),
                     reads=[f"lat_b{i}"], writes=[f"lat_g{i}"], dma="cc")

            ckpt(4)
            S.barrier(lambda e: e.memset(stat[:, 20:21], 0.0), skip_cc=True)
            S.op("gpsimd", lambda e: e.memset(stat[:, 26:27], 0.0), writes=["QTGATE"])
            for jp in range(4):
                for k in range(2):
                    dma("sync", LAT[:, k, :].rearrange("p (c j t) -> p c j t", c=4, j=4)[:, :, jp, :],
                        lat_g[k].ap()[jp * 128:(jp + 1) * 128, :].rearrange("p (c t) -> p c t", c=4),
                        reads=[f"lat_g{k}"], writes=["LAT"])
            qit = 0
            for c in range(NB):
                for h in range(12):
                    hc = slice(h * 96, (h + 1) * 96)
                    ta, tat = TMP[2 * (qit % 2)], f"TMP{2 * (qit % 2)}"
                    tb_, tbt_ = TMP[2 * (qit % 2) + 1], f"TMP{2 * (qit % 2) + 1}"
                    qit += 1
                    pb, pbt = nb()
                    mm(pb[0:96, :], [(WUQ[:, k, hc], CQT[:, k, blk(c)]) for k in range(3)], ["WSM", f"CQT{c}"], [pbt])
                    pb2, pbt2 = nb()
                    mm(pb2[0:96, :], [(WUQS[:, k, hc], CQT[:, k, blk(c)]) for k in range(3)], ["WSMs", f"CQT{c}"], [pbt2])
                    S.op("scalar", lambda e, pb=pb, h=h, c=c: e.copy(out=QT[0:64, h, blk(c)], in_=pb[0:64, :]), reads=[pbt, "QTGATE"], writes=[f"QT{c}"], merge=True)
                    tt("vector", ta[R, :], pb[R, :], CTAB[R, blk(c)], ALU.mult, [pbt, "CTAB"], [tat])
                    tt("vector", tb_[R, :], pb2[R, :], STAB[R, blk(c)], ALU.mult, [pbt2, "STAB"], [tbt_])
                    S.op("gpsimd", lambda e, h=h, c=c, ta=ta, tb_=tb_: e.tensor_tensor(out=QT[R, h, blk(c)], in0=ta[R, :], in1=tb_[R, :], op=ALU.add),
                         reads=[tat, tbt_, "QTGATE"], writes=[f"QT{c}"], merge=True)


            ckpt(5, lambda: {"QT": (QT[0:96, 0:4, 0:512], slice(0, 96), 2048, 0), "G": (G[:, :, 0:512], slice(0, 128), 4096, 3072)})
            S.barrier(lambda e: e.memset(stat[:, 21:22], 0.0))
            for i in range(2):
                S.op("gpsimd", lambda e, i=i: e.memset(VA[i][:, :, 64:66], 1.0), writes=[f"VA{i}"])
            for jp in range(4):
                for i in range(1):
                    dma("sync", KT[i][64:96, :].rearrange("p (c j t) -> p c j t", c=4, j=4)[:, :, jp, :],
                        lat_g[2].ap()[jp * 32:(jp + 1) * 32, :].rearrange("p (c t) -> p c t", c=4),
                        reads=["lat_g2"], writes=[f"KTr{i}"])
            WUKV = WSM[:].rearrange("p k c -> p (k c)")[:, 0:3072].rearrange("p (k c) -> p k c", k=2)
            wload(WUKV, w_ukv, ["WSM", "WSMs"])

            def wukv(k, col):
                return WUKV[:, k, col], "WSM"

            ckpt(6, lambda: {"LAT": (LAT[:, 0, 0:4096], slice(0, 128), 4096, 0), "KT": (KT[0][64:96, 0:4096], slice(64, 96), 4096, 4096)})
            deferred_fin = []
            for h in range(NHEADS_DBG):
                b_ = h % 2
                for kb in range(16):
                    pb, pbt = nb((5, 6))
                    wc = slice(h * 128, h * 128 + 64)
                    mm(pb[0:64, :], [(wukv(k, wc)[0], LAT[:, k, blk(kb)]) for k in range(2)], [wukv(0, wc)[1], "LAT"], [pbt])
                    if kb % 2 == 0:
                        vcopy("vector", KT[0][0:64, blk(kb)], pb[0:64, :], [pbt], [f"KT0_{kb}"])
                    else:
                        S.op("scalar", lambda e, pb=pb, b_=b_, kb=kb: e.copy(out=KT[0][0:64, blk(kb)], in_=pb[0:64, :]), reads=[pbt], writes=[f"KT0_{kb}"])
                for g8 in range(8):
                    pb, pbt = nb((5, 6))
                    wc = slice(h * 128 + 64, h * 128 + 128)

                    def fn(e, pb=pb, g8=g8, wc=wc):
                        ins = None
                        for i in range(8):
                            kt = g8 * 8 + i
                            for k in range(2):
                                ins = e.matmul(pb[:, i * 64:(i + 1) * 64], lhsT=LAT[:, k, kt * 128:(kt + 1) * 128], rhs=wukv(k, wc)[0], start=(k == 0), stop=(k == 1))
                        return ins
                    S.op("tensor", fn, reads=["LAT", wukv(0, wc)[1]], writes=[pbt])
                    vcopy("vector", VA[b_][:, g8 * 8:(g8 + 1) * 8, 0:64], pb[:].rearrange("p (t d) -> p t d", t=8), [pbt, f"VA{b_}"], [f"VA{b_}_{g8}"])
                if h == NHEADS_DBG - 1:
                    wload(WSM[:], w_out[0], ["WSM", "WSMs"])
                for c in range(NB):
                    nkb = 4 * c + 4
                    tiles = []
                    for kb in range(nkb):
                        for q4 in range(4):
                            kt = kb * 4 + q4
                            mk = MSK[:, (kb - 4 * c) * 4 + q4, :] if kb >= 4 * c else None
                            tiles.append((KT[0][0:96, kt * 128:(kt + 1) * 128], [f"KT0_{kb}", "KTr0"], VA[b_][:, kt, :], f"VA{b_}_{kt // 8}", mk))
                    ob = banks[3 + (h * NB + c) % 2]
                    obt = f"bank{3 + (h * NB + c) % 2}"
                    attend(tiles, QT[0:96, h, blk(c)], f"QT{c}", SC_ATT, ob, obt, 96,
                           lambda ob=ob, obt=obt, h=h, c=c: finalize_head(ob, obt, h // 2, h % 2, c),
                           sb_list=(0, 1, 2, 7), npt=6, LA=3, defer=deferred_fin)
            while deferred_fin:
                deferred_fin.pop(0)()

            ckpt(7, lambda: {"G": (G[:, :, 0:512], slice(0, 128), 4096, 0), "G3": (G[:, :, 1536:2048], slice(0, 128), 4096, 4096)})
            S.barrier(lambda e: e.memset(stat[:, 23:24], 0.0))
            out_ln(0, x, hs.ap(), True, w_prefetched=True)
            S.barrier(lambda e: e.memset(stat[:, 24:25], 0.0))
            ckpt(8)
            final_ops.clear()

            WU1 = WSM[:].rearrange("p k c -> p (k c)")[:, 0:6144].rearrange("p (k c) -> p k c", k=8)
            wload(WU1, w_in1[:, 0:768], ["WSM", "WSMs"])
            mem_kv(1)
            wload(WG1, w_in1[:, 768:2048], ["WIN", "WMEM", "MT", "TB0", "TB1"])
            dma("sync", L1C[:, 0:48], l1pack[:, :], writes=["L1C"])
            dma("gpsimd", WR[:], w_rbd[:, :, :], writes=["WR"])
            dma("gpsimd", WI[:], w_ibd[:, :, :], writes=["WI"])
            act(L1C[:, 48:54], L1C[:, 42:48], AF.Exp, ["L1C"], ["L1C"], scale=-1.0)
            act(L1C[:, 48:54], L1C[:, 48:54], AF.Ln, ["L1C"], ["L1C"], bias=1.0, scale=1.0)
            ts("vector", L1C[:, 54:60], L1C[:, 48:54], -16.0, None, ALU.mult, None, ["L1C"], ["L1C"])
            ts("vector", L1C[:, 48:54], L1C[:, 48:54], -8.0, None, ALU.mult, None, ["L1C"], ["L1C"])

            UT = sb("UT", [128, 6, 16], F32)
            S.op("gpsimd", lambda e: e.memset(UT[:], 0.0), writes=["UT"])
            for c in range(NB):
                for ch in range(6):
                    pb, pbt = nb()
                    mm(pb[:, 0:3], [(WU1[:, k, ch * 128:(ch + 1) * 128], XT[:, k, c * 512 + 509:c * 512 + 512]) for k in range(8)], ["WSM", f"XT{c}"], [pbt])
                    vcopy("vector", UT[:, ch, c * 4:c * 4 + 3], pb[:, 0:3], [pbt], ["UT"])
            dma("sync", uh_b.ap().rearrange("(k p) t -> p k t", p=128), UT[:], reads=["UT"], writes=["uh_b"])
            S.op("gpsimd", lambda e: e.collective_compute("AllGather", ALU.bypass, replica_groups=groups, ins=[uh_b.ap().opt()], outs=[uh_g.ap().opt()]),
                 reads=["uh_b"], writes=["uh_g"], dma="cc")
            HALL = sb("HALL", [128, 6, 4, 16], F32)
            for jp in range(4):
                dma("sync", HALL[:, :, jp, :], uh_g.ap()[jp * 768:(jp + 1) * 768, :].rearrange("(k p) t -> p k t", p=128), reads=["uh_g"], writes=["HALL"])
            S.op("gpsimd", lambda e: e.memset(HALO[:], 0.0), writes=["HALO"])
            for c in range(NB):
                for jj in range(4):
                    if jj == 0 and c == 0:
                        continue
                    src = HALL[:, :, jj - 1, c * 4:c * 4 + 3] if jj >= 1 else HALL[:, :, 3, (c - 1) * 4:(c - 1) * 4 + 3]
                    stt(HALO[:, :, c * 4:c * 4 + 3], src, SEL[:, jj:jj + 1], HALO[:, :, c * 4:c * 4 + 3], ALU.mult, ALU.add, ["HALL", "SEL", "HALO"], ["HALO"])

            ckpt(9)
            TMPB = [view(A4, 10240 + i * 2048, 2048, F32) for i in range(4)] + \
                   [MSK[:].rearrange("p a b -> p (a b)")[:, i * 2048:(i + 1) * 2048].bitcast(F32) for i in range(4)]
            ZER = TMPB[6]
            S.op("gpsimd", lambda e: e.memset(ZER[:], 0.0), writes=["ZER"])
            def ctx(it):
                c, ch = it // 6, it % 6

                def T(k):
                    return (TMP[k], f"TMP{k}") if it % 2 == 0 else (TMPB[k], f"TMPB{k}")
                return c, ch, T, (UX[it % 2], f"UX{it % 2}"), (XCB[it % 2], f"XCB{it % 2}")

            gate_ps = {}

            def stA(it):
                c, ch, T, (ux, uxt), (xcb, xcbt) = ctx(it)
                pb, pbt = proj_chunk(c, slice(ch * 128, (ch + 1) * 128), w=WU1, wtok="WSM")
                S.op("scalar", lambda e, ux=ux, pb=pb: e.copy(out=ux[:, 3:515], in_=pb[:]), reads=[pbt], writes=[uxt])
                vcopy("vector", ux[:, 0:3], HALO[:, ch, c * 4:c * 4 + 3], ["HALO", uxt], [uxt])
                xc, xct = T(0)
                cw = lambda tap: L1C[:, ch * 4 + tap:ch * 4 + tap + 1]
                ts("vector", xc[:], ux[:, 0:512], cw(0), L1C[:, 24 + ch:25 + ch], ALU.mult, ALU.add, [uxt, "L1C"], [xct])
                for tap in range(1, 4):
                    stt(xc[:], ux[:, tap:tap + 512], cw(tap), xc[:], ALU.mult, ALU.add, [uxt, "L1C", xct], [xct])
                vcopy("vector", xcb[:], xc[:], [xct], [xcbt])
                pr, prt = nb()
                mm(pr[:], [(WR[:, ch, :], xcb[:])], ["WR", xcbt], [prt])
                pi_, pit = nb()
                mm(pi_[:], [(WI[:, ch, :], xcb[:])], ["WI", xcbt], [pit])
                gate_ps[it] = (pr, prt, pi_, pit)

            def stB(it):
                c, ch, T, _, _ = ctx(it)
                pr, prt, pi_, pit = gate_ps[it]
                r_, rt = T(1)
                i_, itk = T(2)
                a_, at_ = T(3)
                m_, mt_ = T(4)
                act(r_[:], pr[:], AF.Sigmoid, [prt, "L1C"], [rt], bias=L1C[:, 30 + ch:31 + ch])
                act(i_[:], pi_[:], AF.Sigmoid, [pit, "L1C"], [itk], bias=L1C[:, 36 + ch:37 + ch])
                act(a_[:], r_[:], AF.Exp, [rt, "L1C"], [at_], scale=L1C[:, 48 + ch:49 + ch])
                act(m_[:], r_[:], AF.Exp, [rt, "L1C"], [mt_], scale=L1C[:, 54 + ch:55 + ch])
                act(m_[:], m_[:], AF.Sqrt, [mt_], [mt_], scale=-1.0, bias=1.0)

            def stC(it):
                c, ch, T, _, _ = ctx(it)
                xc, xct = T(0)
                i_, itk = T(2)
                a_, at_ = T(3)
                m_, mt_ = T(4)
                tt("gpsimd", i_[:], i_[:], xc[:], ALU.mult, [itk, xct], [itk])
                tt("gpsimd", i_[:], i_[:], m_[:], ALU.mult, [itk, mt_], [itk])
                hl, hlt = T(5)
                pp, ppt = T(1)
                S.op("vector", lambda e, hl=hl, a_=a_, i_=i_: e.tensor_tensor_scan(out=hl[:], data0=a_[:], data1=i_[:], initial=0.0, op0=ALU.mult, op1=ALU.add), reads=[at_, itk], writes=[hlt])
                S.op("vector", lambda e, pp=pp, a_=a_: e.tensor_tensor_scan(out=pp[:], data0=a_[:], data1=ZER[:], initial=1.0, op0=ALU.mult, op1=ALU.add), reads=[at_, "ZER"], writes=[ppt])
                vcopy("gpsimd", HL[:, ch, blk(c)], hl[:], [hlt], [f"HL{c}"])
                vcopy("gpsimd", PP[:, ch, blk(c)], pp[:], [ppt], [f"PP{c}"])
                vcopy("vector", SUMS[:, ch, c * 2:c * 2 + 1], pp[:, 511:512], [ppt], ["SUMS"])
                vcopy("vector", SUMS[:, ch, c * 2 + 1:c * 2 + 2], hl[:, 511:512], [hlt], ["SUMS"])

            NIT = NB * 6
            stA(0)
            for it in range(NIT):
                if it + 1 < NIT:
                    stA(it + 1)
                stB(it)
                stC(it)
            dma("sync", sm_b.ap().rearrange("(k p) t -> p k t", p=128), SUMS[:, :, 0:8], reads=["SUMS"], writes=["sm_b"])
            S.op("gpsimd", lambda e: e.collective_compute("AllGather", ALU.bypass, replica_groups=groups, ins=[sm_b.ap().opt()], outs=[sm_g.ap().opt()]),
                 reads=["sm_b"], writes=["sm_g"], dma="cc")
            ckpt(10)
            wload(WSM[:], w_out[1], ["WSM", "WSMs"])
            SALL = sb("SALL", [128, 6, 4, 8], F32)
            STT = sb("STT", [128, 6, 17], F32)
            HIN = sb("HIN", [128, 6, 4], F32)

            def carry_states():
                for jp in range(4):
                    dma("sync", SALL[:, :, jp, :], sm_g.ap()[jp * 768:(jp + 1) * 768, :].rearrange("(k p) t -> p k t", p=128), reads=["sm_g"], writes=["SALL"])
                S.op("gpsimd", lambda e: e.memset(STT[:], 0.0), writes=["STT"])
                for g in range(16):
                    cc_, jj = g // 4, g % 4
                    tt("vector", STT[:, :, g + 1:g + 2], SALL[:, :, jj, cc_ * 2:cc_ * 2 + 1], STT[:, :, g:g + 1], ALU.mult, ["SALL", "STT"], ["STT"])
                    tt("vector", STT[:, :, g + 1:g + 2], STT[:, :, g + 1:g + 2], SALL[:, :, jj, cc_ * 2 + 1:cc_ * 2 + 2], ALU.add, ["SALL", "STT"], ["STT"])
                S.op("gpsimd", lambda e: e.memset(HIN[:], 0.0), writes=["HIN"])
                for c in range(NB):
                    for jj in range(4):
                        stt(HIN[:, :, c:c + 1], STT[:, :, 4 * c + jj:4 * c + jj + 1], SEL[:, jj:jj + 1], HIN[:, :, c:c + 1], ALU.mult, ALU.add, ["STT", "SEL", "HIN"], ["HIN"])

            def fixup(c):
                for ch in range(6):
                    t0, t0t = TMP[(c * 6 + ch) % 2], f"TMP{(c * 6 + ch) % 2}"
                    stt(t0[:], PP[:, ch, blk(c)], HIN[:, ch, c:c + 1], HL[:, ch, blk(c)], ALU.mult, ALU.add, [f"PP{c}", f"HL{c}", "HIN"], [t0t])
                    tt("gpsimd", G[:, ch, blk(c)], t0[:], G[:, ch, blk(c)], ALU.mult, [t0t, f"G{ch}_{c}"], [f"G{ch}_{c}"])

            for c in range(NB):
                gate_qmem(c, 0, w=WG1, wtok="WIN")
                if c == 0:
                    carry_states()
                fixup(c)
            ckpt(11)
            S.barrier(lambda e: e.memset(stat[:, 25:26], 0.0))
            out_ln(1, hs.ap(), out, False, w_prefetched=True)
        except _Stop:
            pass
        if not final_ops:
            for name, (ap, rows, ncols, off) in (DUMPS.items() if not os.environ.get('KNODUMP') else []):
                dst = dbgout[rows, off:off + ncols]
                if len(ap.shape) == 3:
                    dst = dst.rearrange("p (k t) -> p k t", k=ap.shape[1])
                final_ops.append(dma("gpsimd", dst, ap, reads=list(S.last_w.keys())))
            final_ops.append(dma("sync", out[0:128, :], x[0:128, :]))
        S.emit(final_wait_ops=list(final_ops))
    return nc


_NC_CACHE = {}


def _prep_inputs(inp):
    f32 = np.float32
    x = np.asarray(inp["x"], f32)
    mem = np.asarray(inp["mem"], f32)
    pos = np.asarray(inp["positions"], np.int32)
    w_in0 = np.ascontiguousarray(np.asarray(inp["mla_w_in"], f32)[0])
    kr = w_in0[:, 640:672]
    w_krs = np.ascontiguousarray(np.concatenate([kr[:, 16:32], kr[:, 0:16]], axis=1))
    w_uq = np.ascontiguousarray(np.asarray(inp["mla_w_uq"], f32)[0])
    wq3 = w_uq.reshape(384, 12, 96)
    w_uqs = np.ascontiguousarray(np.concatenate([wq3[:, :, 0:64], wq3[:, :, 80:96], wq3[:, :, 64:80]], axis=2).reshape(384, 1152))
    qn = np.ascontiguousarray(np.asarray(inp["mla_q_norm"], f32)[0].reshape(3, 128).T)
    kvn = np.ascontiguousarray(np.asarray(inp["mla_kv_norm"], f32)[0].reshape(2, 128).T)
    w_ukv = np.ascontiguousarray(np.asarray(inp["mla_w_ukv"], f32)[0])
    w_in1 = np.ascontiguousarray(np.asarray(inp["lru_w_in"], f32)[0])
    cw = np.asarray(inp["lru_conv_w"], f32)[0]
    conv_w = np.ascontiguousarray(cw.reshape(4, 6, 128).transpose(2, 1, 0))
    pp = lambda v: np.ascontiguousarray(np.asarray(v, f32)[0].reshape(6, 128).T)
    conv_b = pp(inp["lru_conv_b"])
    b_r = pp(inp["lru_b_rgate"])
    b_i = pp(inp["lru_b_igate"])
    lam = pp(inp["lru_lambda"])

    def blockdiag(w):
        w = np.asarray(w, f32)[0]
        o = np.zeros((128, 6, 128), f32)
        for g in range(12):
            ch, e = g // 2, g % 2
            o[e * 64:(e + 1) * 64, ch, e * 64:(e + 1) * 64] = w[g]
        return o
    w_rbd = blockdiag(inp["lru_w_rgate"])
    w_ibd = blockdiag(inp["lru_w_igate"])
    w_mem = np.ascontiguousarray(np.asarray(inp["w_mem_kv"], f32))
    w_out = np.ascontiguousarray(np.asarray(inp["w_out"], f32))
    ln_g = np.ascontiguousarray(np.asarray(inp["ln_g"], f32))
    ln_b = np.ascontiguousarray(np.asarray(inp["ln_b"], f32))
    invf16 = (np.float32(10000.0) ** (-np.arange(16, dtype=np.float32) / np.float32(16))).astype(f32)
    invf = np.concatenate([invf16, invf16]).reshape(32, 1).astype(f32)
    sgn = np.concatenate([-np.ones(16, f32), np.ones(16, f32)]).reshape(32, 1)
    l1pack = np.ascontiguousarray(np.concatenate([conv_w.reshape(128, 24), conv_b, b_r, b_i, lam], axis=1)).astype(f32)
    shared = dict(w_in0=w_in0, w_krs=w_krs, w_uq=w_uq, w_uqs=w_uqs, w_ukv=w_ukv, w_in1=w_in1,
                  w_rbd=w_rbd, w_ibd=w_ibd, l1pack=l1pack,
                  w_mem=w_mem, w_out=w_out, ln_g=ln_g, ln_b=ln_b)
    maps = []
    kk = np.arange(128)[:, None]
    qq = np.arange(512)[None, :]
    for r in range(8):
        b, j = r // 4, r % 4
        tok = np.concatenate([np.arange((4 * c + j) * 512, (4 * c + j + 1) * 512) for c in range(4)])
        m = dict(shared)
        m["x"] = np.ascontiguousarray(x[b, tok])
        m["posb"] = np.ascontiguousarray(np.broadcast_to(pos[b, tok][None, :], (32, NT))).astype(np.int32)
        m["mem"] = np.ascontiguousarray(mem[b])
        msk = np.zeros((128, 16, 512), np.uint8)
        for g in range(4):
            for q4 in range(4):
                if g < j:
                    msk[:, g * 4 + q4, :] = 1
                elif g == j:
                    msk[:, g * 4 + q4, :] = ((q4 * 128 + kk) <= qq).astype(np.uint8)
        m["masks"] = msk
        cp = np.zeros((128, 64), f32)
        cp[:, 0:3] = qn
        cp[:, 3:5] = kvn
        cp[64:96, 5:6] = invf
        cp[64:96, 6:7] = sgn
        cp[:, 8 + j] = 1.0
        m["cpack"] = cp
        maps.append(m)
    return maps


def kernel(**inputs):
    if "nc" not in _NC_CACHE:
        _NC_CACHE["nc"] = build()
    nc = _NC_CACHE["nc"]
    maps = _prep_inputs(inputs)
    res = run_bass_kernel_spmd(nc, maps, core_ids=list(range(8)))
    outp = np.zeros((2, SEQ, 1024), np.float32)
    for r in range(8):
        b, j = r // 4, r % 4
        o = np.asarray(res.results[r]["out"], np.float32)
        for c in range(4):
            outp[b, (4 * c + j) * 512:(4 * c + j + 1) * 512] = o[c * 512:(c + 1) * 512]
    return outp
```

```python
import contextlib
import os
import math
import numpy as np
import concourse.bass as bass
import concourse.mybir as mybir
from concourse.bass_utils import run_bass_kernel_spmd

F32 = mybir.dt.float32
BF16 = mybir.dt.bfloat16
U8 = mybir.dt.uint8
I32 = mybir.dt.int32
AF = mybir.ActivationFunctionType
ALU = mybir.AluOpType

PHASE = 3000
COMPUTE = ("tensor", "vector", "scalar", "gpsimd")


class Op:
    __slots__ = ("eng", "fn", "deps", "idx", "is_dma", "id")


class Sched:
    def __init__(self, nc, ndma_sems=16):
        self.nc = nc
        self.ops = []
        self.per_eng = {e: [] for e in ("tensor", "vector", "scalar", "gpsimd", "sync")}
        self.last_w = {}
        self.readers = {}
        self.ndma_sems = ndma_sems
        self.dma_count = {}
        self.ccount = {}
        self.cc_ops = []
        self.barrier_dep = None

    def op(self, eng, fn, reads=(), writes=(), dma=False, extra_deps=(), merge=False):
        o = Op()
        o.eng = eng
        o.fn = fn
        o.is_dma = dma
        o.id = len(self.ops)
        deps = set(extra_deps)
        if self.barrier_dep is not None:
            deps.add(self.barrier_dep)
        for t in list(reads) + ([] if merge else list(writes)):
            for w in self.last_w.get(t, ()):
                deps.add(w)
        for t in writes:
            for r in self.readers.get(t, {}).values():
                deps.add(r)
        for t in reads:
            if t.startswith("bank"):
                for r in self.readers.get(t, {}).values():
                    deps.add(r)
        o.deps = sorted(deps)
        for t in writes:
            if merge:
                self.last_w[t] = set(self.last_w.get(t, ())) | {o.id}
            else:
                self.last_w[t] = {o.id}
                self.readers[t] = {}
        for t in reads:
            key = (eng,) if not dma else (eng, o.id)
            self.readers.setdefault(t, {})[key] = o.id
        if dma == "cc":
            o.idx = len(self.cc_ops)
            self.cc_ops.append(o)
        elif dma:
            n = self.dma_count.get(eng, 0)
            self.dma_count[eng] = n + 1
            o.idx = n
        else:
            o.idx = self.ccount.get(eng, 0)
            self.ccount[eng] = o.idx + 1
        self.per_eng[eng].append(o)
        self.ops.append(o)
        return o.id

    def barrier(self, mk, skip_cc=False, skip_dma=False):
        deps = set()
        for e, lst in self.per_eng.items():
            last_c = None
            for o in lst:
                if o.is_dma == "cc" and skip_cc:
                    continue
                if o.is_dma and skip_dma:
                    continue
                if o.is_dma:
                    deps.add(o.id)
                else:
                    last_c = o.id
            if last_c is not None:
                deps.add(last_c)
        b = self.op("gpsimd", mk, extra_deps=deps)
        self.barrier_dep = b
        return b

    def emit(self, final_wait_ops=()):
        nc = self.nc
        with contextlib.ExitStack() as st:
            csem = {}
            for e in COMPUTE:
                n = self.ccount.get(e, 0)
                nph = max(1, (n + PHASE - 1) // PHASE)
                csem[e] = [st.enter_context(nc.semaphore(f"c_{e}_{i}")) for i in range(nph)]
            dsem = {}
            for e, n in self.dma_count.items():
                k = min(self.ndma_sems, n)
                dsem[e] = [st.enter_context(nc.semaphore(f"d_{e}_{i}")) for i in range(k)]
            ccsem = [st.enter_context(nc.semaphore(f"cc_{i}")) for i in range(len(self.cc_ops))]
            block = st.enter_context(nc.Block())
            ops = self.ops

            def target(o):
                if o.is_dma == "cc":
                    return ccsem[o.idx], 1
                if o.is_dma:
                    k = len(dsem[o.eng])
                    return dsem[o.eng][o.idx % k], 16 * (o.idx // k + 1)
                return csem[o.eng][o.idx // PHASE], (o.idx % PHASE) + 1

            def run_engine(ename, eng):
                waited = {}
                waited_dma = set()
                for o in self.per_eng[ename]:
                    for d in o.deps:
                        p = ops[d]
                        if p.is_dma:
                            if d in waited_dma:
                                continue
                            waited_dma.add(d)
                            s, v = target(p)
                            eng.wait_ge(s, v)
                        else:
                            if p.eng == ename and ename == "tensor":
                                continue
                            if waited.get(p.eng, -1) >= p.idx:
                                continue
                            waited[p.eng] = p.idx
                            s, v = target(p)
                            eng.wait_ge(s, v)
                    if o.is_dma == "cc":
                        s, v = target(o)
                        o.fn(eng).then_inc(s)
                    elif o.is_dma:
                        k = len(dsem[o.eng])
                        if o.idx >= k:
                            eng.wait_ge(dsem[o.eng][o.idx % k], 16 * (o.idx // k))
                        s, v = target(o)
                        o.fn(eng).then_inc(s, 16)
                    else:
                        s, v = target(o)
                        o.fn(eng).then_inc(s, 1)
                if ename == "sync":
                    for fo in final_wait_ops:
                        s, v = target(ops[fo])
                        eng.wait_ge(s, v)

            @block.sync
            def _(e):
                run_engine("sync", e)

            @block.tensor
            def _(e):
                run_engine("tensor", e)

            @block.vector
            def _(e):
                run_engine("vector", e)

            @block.scalar
            def _(e):
                run_engine("scalar", e)

            @block.gpsimd
            def _(e):
                run_engine("gpsimd", e)


NT = 2048
NB = 4
TT = 16
SEQ = 8192
ALPHA = 4.0 ** 0.25
EPS = 1e-6
SC_ATT = 1.0 / math.sqrt(96.0)
SC_MEM = 1.0 / 8.0
TWO_PI = float(2.0 * np.pi)
DEBUG = False
NHEADS_DBG = int(os.environ.get("KHEADS", "12"))
KF = int(os.environ.get("KF", "9"))


class _Stop(Exception):
    pass


STAGE = int(os.environ.get("KSTAGE", "99"))


def build():
    nc = bass.Bass("TRN2", target_bir_lowering=False)
    DUMPS = {}

    def ckpt(n, dumps=None):
        if STAGE == n:
            DUMPS.update(dumps() if dumps else {})
            raise _Stop()

    def din(name, shape, dt=F32):
        return nc.dram_tensor(name, list(shape), dt, kind="ExternalInput").ap()

    x = din("x", [NT, 1024])
    posb = din("posb", [32, NT], I32)
    cpack = din("cpack", [128, 64])
    l1pack = din("l1pack", [128, 48])
    mem = din("mem", [256, 1024])
    masks = din("masks", [128, 16, 512], U8)
    w_in0 = din("w_in0", [1024, 1952])
    w_krs = din("w_krs", [1024, 32])
    w_uq = din("w_uq", [384, 1152])
    w_uqs = din("w_uqs", [384, 1152])
    w_ukv = din("w_ukv", [256, 1536])
    w_in1 = din("w_in1", [1024, 2048])
    w_rbd = din("w_rbd", [128, 6, 128])
    w_ibd = din("w_ibd", [128, 6, 128])
    w_mem = din("w_mem", [2, 1024, 512])
    w_out = din("w_out", [2, 1024, 1024])
    ln_g = din("ln_g", [2, 1024])
    ln_b = din("ln_b", [2, 1024])
    out = nc.dram_tensor("out", [NT, 1024], F32, kind="ExternalOutput").ap()
    dbgout = nc.dram_tensor("dbgout", [128, 8192], F32, kind="ExternalOutput").ap() if STAGE != 99 else None

    lat_b = [nc.dram_tensor(f"lat_b{i}", [n, NT], BF16) for i, n in enumerate((128, 128, 32))]
    lat_g = [nc.dram_tensor(f"lat_g{i}", [4 * n, NT], BF16) for i, n in enumerate((128, 128, 32))]
    hs = nc.dram_tensor("hs", [NT, 1024], F32)
    uh_b = nc.dram_tensor("uh_b", [768, 16], F32)
    uh_g = nc.dram_tensor("uh_g", [4 * 768, 16], F32)
    sm_b = nc.dram_tensor("sm_b", [768, 8], F32)
    sm_g = nc.dram_tensor("sm_g", [4 * 768, 8], F32)
    groups = [[0, 1, 2, 3], [4, 5, 6, 7]]

    with contextlib.ExitStack() as st:
        def sb(name, shape, dt):
            return st.enter_context(nc.sbuf_tensor(name, list(shape), dt))

        banks = [st.enter_context(nc.psum_tensor(f"pb{i}", [128, 512], F32)) for i in range(8)]
        S = Sched(nc)

        A1 = sb("A1", [128, 32 * 1024], U8)
        A2 = sb("A2", [128, 24 * 1024], U8)
        A3 = sb("A3", [128, 48 * 1024], U8)
        A4 = sb("A4", [128, 20 * 1024], U8)

        def view(arena, off, nbytes, dt, pat=None, **kw):
            v = arena[:, off:off + nbytes].bitcast(dt)
            if pat:
                v = v.rearrange(pat, **kw)
            return v

        XT = view(A1, 0, 32768, BF16, "p (k t) -> p k t", k=8)
        LAT = view(A1, 0, 32768, BF16, "p (k t) -> p k t", k=2)
        WIN = view(A3, 0, 32768, BF16, "p (k c) -> p k c", k=8)
        WG1 = view(A3, 24576, 20480, BF16, "p (k c) -> p k c", k=8)
        CTAB = view(A2, 0, 8192, F32)
        STAB = view(A2, 8192, 8192, F32)
        PI32 = view(A4, 0, 8192, I32)
        CT2 = view(A4, 8192, 8192, F32)
        KRo = view(A2, 16384, 4096, BF16)
        WMEM = view(A3, 32768, 8192, BF16, "p (k c) -> p k c", k=8)
        MT = view(A3, 40960, 4096, BF16, "p (k c) -> p k c", k=8)
        TB = [view(A3, 45056 + i * 2048, 2048, BF16) for i in range(2)]
        RB = [view(A3, i * 4096, 4096, F32) for i in range(6)]
        LG = view(A3, 32768, 4096, F32)
        LB = view(A3, 36864, 4096, F32)
        UX = [view(A4, i * 2064, 2064, F32) for i in range(2)]
        XCB = [view(A4, 4128 + i * 1024, 1024, BF16) for i in range(2)]
        WR = view(A4, 6176, 1536, BF16, "p (k c) -> p k c", k=6)
        WI = view(A4, 7712, 1536, BF16, "p (k c) -> p k c", k=6)
        KT = [view(A2, 0, 16384, BF16)] * 2
        HL = view(A2, 0, 12288 * 2, BF16, "p (k t) -> p k t", k=6)
        PP = view(A3, 0, 12288 * 2, BF16, "p (k t) -> p k t", k=6)
        QT = view(A3, 0, 49152, BF16, "p (h t) -> p h t", h=12)
        CQT = view(A4, 0, 12288, BF16, "p (k t) -> p k t", k=3)
        LATo = view(A4, 12288, 8192, BF16, "p (k t) -> p k t", k=2)
        VA = [view(A4, i * 8448, 64 * 66 * 2, BF16, "p (t c) -> p t c", c=66) for i in range(2)]

        G = sb("G", [128, 8, NT], BF16)
        QMB = [sb("QMB0", [128, 2, 512], BF16)] * 2
        WSM = sb("WSM", [128, 8, 1024], BF16)
        WKS = sb("WKS", [128, 8, 96], BF16)
        MSK = sb("MSK", [128, 16, 512], U8)
        PT = [sb(f"PT{i}", [128, 512], BF16) for i in range(4)]
        ident = sb("ident", [128, 128], BF16)
        identf = sb("identf", [128, 128], F32)
        ones_bf = sb("ones_bf", [128, 128], BF16)
        ones_f = sb("ones_f", [128, 128], F32)
        smallc = sb("smallc", [128, 64], F32)
        MK = sb("MK", [128, 2, 256], BF16)
        MVA = sb("MVA", [128, 4, 2, 66], BF16)
        TMP = [sb(f"TMP{i}", [128, 512], F32) for i in range(6)]
        RD = TMP[4]
        stat = sb("stat", [128, 32], F32)
        STATS = [stat[:, 0:17], smallc[:, 16:33], smallc[:, 33:50]]
        L1C = sb("L1C", [128, 64], F32)
        HALO = sb("HALO", [128, 6, 16], F32)
        SUMS = sb("SUMS", [128, 6, 8], F32)
        dbg = None

        bank_rr = [0]

        def nb(lst=(0, 1, 2, 3, 4, 5, 6, 7)):
            i = lst[bank_rr[0] % len(lst)]
            bank_rr[0] += 1
            return banks[i], f"bank{i}"

        def mm(out_ap, pairs, reads, writes):
            def fn(e):
                n = len(pairs)
                ins = None
                for i, (l, r) in enumerate(pairs):
                    ins = e.matmul(out_ap, lhsT=l, rhs=r, start=(i == 0), stop=(i == n - 1))
                return ins
            return S.op("tensor", fn, reads=reads, writes=writes)

        def dma(eng, out_ap, in_ap, reads=(), writes=(), extra_deps=(), merge=False, **kw):
            return S.op(eng, lambda e: e.dma_start(out=out_ap, in_=in_ap, **kw), reads=reads, writes=writes, dma=True,
                        extra_deps=extra_deps, merge=merge)

        def act(out_ap, in_ap, func, reads, writes, **kw):
            return S.op("scalar", lambda e: e.activation(out=out_ap, in_=in_ap, func=func, **kw), reads=reads, writes=writes)

        def vcopy(eng, out_ap, in_ap, reads, writes):
            return S.op(eng, lambda e: e.tensor_copy(out=out_ap, in_=in_ap), reads=reads, writes=writes)

        def tt(eng, out_ap, a, b, op, reads, writes):
            return S.op(eng, lambda e: e.tensor_tensor(out=out_ap, in0=a, in1=b, op=op), reads=reads, writes=writes)

        def ts(eng, out_ap, a, s1, s2, op0, op1, reads, writes):
            if op1 is None:
                return S.op(eng, lambda e: e.tensor_scalar(out=out_ap, in0=a, scalar1=s1, scalar2=None, op0=op0), reads=reads, writes=writes)
            return S.op(eng, lambda e: e.tensor_scalar(out=out_ap, in0=a, scalar1=s1, scalar2=s2, op0=op0, op1=op1), reads=reads, writes=writes)

        def stt(out_ap, a, s, b, op0, op1, reads, writes):
            return S.op("vector", lambda e: e.scalar_tensor_tensor(out=out_ap, in0=a, scalar=s, in1=b, op0=op0, op1=op1), reads=reads, writes=writes)

        def blk(c):
            return slice(c * 512, (c + 1) * 512)

        def wload(dst3, src2, writes):
            K, C = dst3.shape[1], dst3.shape[2]
            first = None
            for k in range(K):
                for c0 in range(0, C, 1024):
                    c1 = min(C, c0 + 1024)
                    if first is None:
                        first = dma("gpsimd", dst3[:, k, c0:c1], src2[k * 128:(k + 1) * 128, c0:c1], writes=writes)
                    else:
                        dma("gpsimd", dst3[:, k, c0:c1], src2[k * 128:(k + 1) * 128, c0:c1], writes=writes,
                            extra_deps=S.ops[first].deps, merge=True)

        S.op("gpsimd", lambda e: e.memset(identf[:], 1.0), writes=["identf"])
        S.op("gpsimd", lambda e: e.affine_select(out=identf[:], in_=identf[:], pattern=[[-1, 128]], compare_op=ALU.is_equal, fill=0.0, base=0, channel_multiplier=1), writes=["identf"])
        vcopy("vector", ident[:], identf[:], ["identf"], ["ident"])
        S.op("gpsimd", lambda e: e.memset(ones_bf[:], 1.0), writes=["ones_bf"])
        S.op("gpsimd", lambda e: e.memset(ones_f[:], 1.0), writes=["ones_f"])
        S.op("gpsimd", lambda e: e.memset(WKS[:], 0.0), writes=["WKS"])
        S.op("gpsimd", lambda e: e.memset(MVA[:, :, :, 64:66], 1.0), writes=["MVAones"])
        dma("sync", MSK[:], masks[:, :, :], writes=["MSK"])
        dma("sync", smallc[:], cpack[:, :], writes=["qn", "kvn", "invf", "sgn", "SEL"])
        SEL = smallc[:, 8:12]

        dma("sync", PI32[64:96, :], posb[:, :], writes=["PI32"])
        R = slice(64, 96)
        vcopy("vector", CTAB[R, :], PI32[R, :], ["PI32"], ["CTAB"])
        ts("vector", CTAB[R, :], CTAB[R, :], smallc[R, 5:6], None, ALU.mult, None, ["CTAB", "invf"], ["CTAB"])

        H = STAB
        H2 = CT2
        ts("vector", H[R, :], CTAB[R, :], 1.0 / TWO_PI, None, ALU.mult, None, ["CTAB"], ["STAB"])
        vcopy("vector", PI32[R, :], H[R, :], ["STAB"], ["PI32"])
        vcopy("vector", H[R, :], PI32[R, :], ["PI32"], ["STAB"])
        stt(H[R, :], H[R, :], -TWO_PI, CTAB[R, :], ALU.mult, ALU.add, ["STAB", "CTAB"], ["STAB"])
        ts("vector", H[R, :], H[R, :], 0.5, None, ALU.mult, None, ["STAB"], ["STAB"])
        tt("vector", H2[R, :], H[R, :], H[R, :], ALU.mult, ["STAB"], ["CT2"])
        SP = PI32[:].bitcast(F32)
        sc = [-1.0 / 39916800.0, 1.0 / 362880.0, -1.0 / 5040.0, 1.0 / 120.0, -1.0 / 6.0, 1.0]
        ts("vector", SP[R, :], H2[R, :], sc[0], None, ALU.mult, None, ["CT2", "PI32"], ["PI32"])
        for cf in sc[1:5]:
            stt(SP[R, :], SP[R, :], cf, H2[R, :], ALU.add, ALU.mult, ["PI32", "CT2"], ["PI32"])
        stt(SP[R, :], SP[R, :], sc[5], H[R, :], ALU.add, ALU.mult, ["PI32", "STAB"], ["PI32"])
        cc = [1.0 / 479001600.0, -1.0 / 3628800.0, 1.0 / 40320.0, -1.0 / 720.0, 1.0 / 24.0, -0.5]
        ts("vector", CTAB[R, :], H2[R, :], cc[0], None, ALU.mult, None, ["CT2", "STAB", "CTAB"], ["CTAB"])
        for cf in cc[1:6]:
            stt(CTAB[R, :], CTAB[R, :], cf, H2[R, :], ALU.add, ALU.mult, ["CTAB", "CT2"], ["CTAB"])
        ts("vector", CTAB[R, :], CTAB[R, :], 1.0, None, ALU.add, None, ["CTAB"], ["CTAB"])
        stt(STAB[R, :], SP[R, :], 2.0, CTAB[R, :], ALU.mult, ALU.mult, ["PI32", "CTAB", "STAB"], ["STAB"])
        tt("vector", CTAB[R, :], SP[R, :], SP[R, :], ALU.mult, ["PI32", "CTAB", "STAB"], ["CTAB"])
        ts("vector", CTAB[R, :], CTAB[R, :], -2.0, 1.0, ALU.mult, ALU.add, ["CTAB"], ["CTAB"])
        ts("vector", STAB[R, :], STAB[R, :], smallc[R, 6:7], None, ALU.mult, None, ["STAB", "sgn"], ["STAB"])

        def make_xt(src_dram):
            for t in range(TT):
                tb = TB[t % 2]
                dma("gpsimd", tb[:], src_dram[t * 128:(t + 1) * 128, :], writes=[f"TB{t % 2}"])
                pb, pbt = nb()
                pv = pb[:].bitcast(BF16).rearrange("p (k t) -> p k t", k=8)

                def fn(e, tb=tb, pv=pv):
                    ins = None
                    for k in range(8):
                        ins = e.transpose(out=pv[:, k, :], in_=tb[:, k * 128:(k + 1) * 128], identity=ident[:])
                    return ins
                S.op("tensor", fn, reads=[f"TB{t % 2}", "ident"], writes=[pbt])
                eng = "vector" if t % 2 == 0 else "scalar"
                if eng == "vector":
                    vcopy("vector", XT[:, :, t * 128:(t + 1) * 128], pv, [pbt], [f"XT{t // 4}"])
                else:
                    S.op("scalar", lambda e, pv=pv, t=t: e.copy(out=XT[:, :, t * 128:(t + 1) * 128], in_=pv), reads=[pbt], writes=[f"XT{t // 4}"])

        final_ops = []
        try:
            make_xt(x)
            ckpt(1, lambda: {"XT": (XT[:, :, 0:512], slice(0, 128), 4096, 0), "CTAB": (CTAB[64:96, 0:512], slice(64, 96), 512, 4096), "STAB": (STAB[64:96, 0:512], slice(64, 96), 512, 4608)})

            def mem_kv(layer):
                wload(WMEM[:], w_mem[layer], ["WMEM"])
                for mt in range(2):
                    tb = TB[0]
                    dma("gpsimd", tb[:], mem[mt * 128:(mt + 1) * 128, :], writes=["TB0"])
                    pb, pbt = nb()
                    pv = pb[:].bitcast(BF16).rearrange("p (k t) -> p k t", k=8)

                    def fn(e, tb=tb, pv=pv):
                        ins = None
                        for k in range(8):
                            ins = e.transpose(out=pv[:, k, :], in_=tb[:, k * 128:(k + 1) * 128], identity=ident[:])
                        return ins
                    S.op("tensor", fn, reads=["TB0", "ident"], writes=[pbt])
                    vcopy("vector", MT[:, :, mt * 128:(mt + 1) * 128], pv, [pbt], ["MT"])
                for ch in range(2):
                    pb, pbt = nb()
                    mm(pb[:, 0:256], [(WMEM[:, k, ch * 128:(ch + 1) * 128], MT[:, k, :]) for k in range(8)], ["WMEM", "MT"], [pbt])
                    vcopy("vector", MK[:, ch, :], pb[:, 0:256], [pbt], ["MK"])
                for mt in range(2):
                    pb, pbt = nb()
                    mm(pb[:, 0:256], [(MT[:, k, mt * 128:(mt + 1) * 128], WMEM[:, k, 256:512]) for k in range(8)], ["WMEM", "MT"], [pbt])
                    vcopy("vector", MVA[:, :, mt, 0:64], pb[:, 0:256].rearrange("p (h d) -> p h d", h=4), [pbt, "MVAones"], ["MVA"])

            def proj_chunk(c, wcols, k_n=8, w=None, rhs=None, rtok=None, wtok="WIN"):
                pb, pbt = nb()
                w = WIN if w is None else w
                rhs = XT if rhs is None else rhs
                rtok = f"XT{c}" if rtok is None else rtok
                mm(pb[0:(wcols.stop - wcols.start), :], [(w[:, k, wcols], rhs[:, k, blk(c)]) for k in range(k_n)], [wtok, rtok], [pbt])
                return pb, pbt

            def gate_qmem(c, col0, w=None, wtok="WIN", wtoks=None):
                for m in range(8):
                    pb, pbt = proj_chunk(c, slice(col0 + m * 128, col0 + (m + 1) * 128), w=w, wtok=(wtoks[m] if wtoks else wtok))
                    act(G[:, m, blk(c)], pb[:], AF.Silu, [pbt], [f"G{m}_{c}"])
                for m in range(2):
                    pb, pbt = proj_chunk(c, slice(col0 + 1024 + m * 128, col0 + 1024 + (m + 1) * 128), w=w, wtok=(wtoks[8 + m] if wtoks else wtok))
                    vcopy("vector", QMB[c % 2][:, m, :], pb[:], [pbt], ["QMB0"])
                ckpt(31, lambda: {"CQT": (CQT[:, :, 0:512], slice(0, 128), 1536, 0), "LATo": (LATo[:, :, 0:512], slice(0, 128), 1024, 1536),
                         "KRo": (KRo[64:96, 0:512], slice(64, 96), 512, 2560), "G": (G[:, :, 0:512], slice(0, 128), 4096, 3072)})
                mem_attention(c)
                ckpt(32, lambda: {"G": (G[:, :, 0:512], slice(0, 128), 4096, 3072)})

            att_state = {"i": 0}
            PTX = list(PT) + [TMP[5][:, 0:256].bitcast(BF16), TMP[5][:, 256:512].bitcast(BF16)]

            def attend(tiles, q_ap, q_tok, scale, o_bank, o_tok, npart, finalize, sb_list=(0, 1, 2), npt=4, LA=2, defer=None):
                n = len(tiles)
                pend = []
                nsb = len(sb_list)

                def issue_qk(i):
                    kT, ktok, v, vtok, mk = tiles[i]
                    gi = att_state["i"]
                    att_state["i"] += 1
                    pb = banks[sb_list[gi % nsb]]
                    pbt = f"bank{sb_list[gi % nsb]}"
                    mm(pb[:], [(kT, q_ap)], (list(ktok) if isinstance(ktok, list) else [ktok]) + [q_tok], [pbt])
                    pt = PTX[gi % npt]
                    ptt = f"PT{gi % npt}"
                    act(pt[:], pb[:], AF.Exp, [pbt], [ptt], scale=scale)
                    if mk is not None:
                        tt("vector", pt[:], pt[:], mk, ALU.mult, [ptt, "MSK"], [ptt])
                    pend.append((i, pt, ptt, v, vtok))

                def issue_pv():
                    i, pt, ptt, v, vtok = pend.pop(0)
                    S.op("tensor", lambda e: e.matmul(o_bank[0:66, :], lhsT=v, rhs=pt[:], start=(i == 0), stop=(i == n - 1)),
                         reads=[ptt, vtok], writes=[o_tok])

                for i in range(n):
                    issue_qk(i)
                    if i >= LA:
                        issue_pv()
                    if defer is not None and i == 8 and defer:
                        defer.pop(0)()
                while pend:
                    issue_pv()
                if defer is None:
                    finalize()
                else:
                    defer.append(finalize)

            fin_rr = [0]

            def finalize_head(o_bank, o_tok, chunk, par, c, bc=(5, 6), rd_alt=False):
                rows = slice(par * 64, par * 64 + 64)
                gt = f"G{chunk}_{c}"
                rdi = 4 + (fin_rr[0] % 2 if rd_alt else 0)
                RDx, rdt = TMP[rdi], f"TMP{rdi}"
                S.op("vector", lambda e: e.reciprocal(out=RDx[64:65, :], in_=o_bank[64:65, :]), reads=[o_tok], writes=[rdt])
                pb, pbt = nb(bc)
                S.op("tensor", lambda e: e.matmul(pb[:], lhsT=ones_f[64:65, :], rhs=RDx[64:65, :], start=True, stop=True), reads=[rdt, "ones_f"], writes=[pbt])
                i = fin_rr[0] % 2
                fin_rr[0] += 1
                t1, t1t = TMP[i], f"TMP{i}"
                t2, t2t = TMP[2 + i], f"TMP{2 + i}"
                S.op("scalar", lambda e: e.copy(out=t1[rows, :], in_=o_bank[0:64, :]), reads=[o_tok], writes=[t1t])
                tt("vector", t2[rows, :], pb[rows, :], G[rows, chunk, blk(c)], ALU.mult, [pbt, gt], [t2t])
                tt("gpsimd", G[rows, chunk, blk(c)], t1[rows, :], t2[rows, :], ALU.mult, [t1t, t2t], [gt])

            def mem_attention(c):
                if True:
                    fins = []
                    for m in range(4):
                        par = m % 2
                        rows = slice(par * 64, par * 64 + 64)
                        ob = banks[(3, 4, 7, 6)[m]]
                        obt = f"bank{(3, 4, 7, 6)[m]}"
                        tiles = [(MK[rows, m // 2, mt * 128:(mt + 1) * 128], "MK", MVA[:, m, mt, :], "MVA", None) for mt in range(2)]
                        attend(tiles, QMB[c % 2][rows, m // 2, :], "QMB0", SC_MEM, ob, obt, 64,
                               lambda ob=ob, obt=obt, m=m, par=par, c=c: finalize_head(ob, obt, 6 + m // 2, par, c, bc=(5, 0), rd_alt=True),
                               defer=fins)
                    while fins:
                        fins.pop(0)()

            def out_ln(layer, res_src, dst_dram, also_xt, w_prefetched=False):
                if not w_prefetched:
                    wload(WSM[:], w_out[layer], ["WSM", "WSMs"])
                dma("sync", LG[:], ln_g[layer:layer + 1, :].partition_broadcast(128), writes=["LG"])
                dma("sync", LB[:], ln_b[layer:layer + 1, :].partition_broadcast(128), writes=["LB"])
                def load_res(t):
                    dma("sync", RB[t % 6][:], res_src[t * 128:(t + 1) * 128, :], writes=[f"RB{t % 6}"])
                for t in range(6):
                    load_res(t)
                def s1(t):
                    rb, rbt = RB[t % 6], f"RB{t % 6}"
                    stat, stt_ = STATS[t % 3], f"stat{t % 3}"
                    c = t // 4
                    for n_ in range(2):
                        pb, pbt = nb()
                        mm(pb[:], [(G[:, k, t * 128:(t + 1) * 128], WSM[:, k, n_ * 512:(n_ + 1) * 512]) for k in range(8)],
                           [f"G{m}_{c}" for m in range(8)] + ["WSM"], [pbt])
                        stt(rb[:, n_ * 512:(n_ + 1) * 512], rb[:, n_ * 512:(n_ + 1) * 512], ALPHA, pb[:], ALU.mult, ALU.add, [pbt, rbt], [rbt])
                        S.op("vector", lambda e, rb=rb, n_=n_, stat=stat: e.bn_stats(out=stat[:, n_ * 6:(n_ + 1) * 6], in_=rb[:, n_ * 512:(n_ + 1) * 512]), reads=[rbt], writes=[stt_])
                    S.op("vector", lambda e, stat=stat: e.bn_aggr(out=stat[:, 12:14], in_=stat[:, 0:12]), reads=[stt_], writes=[stt_])
                    ts("vector", stat[:, 14:15], stat[:, 13:14], EPS, None, ALU.add, None, [stt_], [stt_])
                    act(stat[:, 15:16], stat[:, 14:15], AF.Ln, [stt_], [stt_])
                    act(stat[:, 16:17], stat[:, 15:16], AF.Exp, [stt_], [stt_], scale=-0.5)

                def s3(t):
                    rb, rbt = RB[t % 6], f"RB{t % 6}"
                    stat, stt_ = STATS[t % 3], f"stat{t % 3}"
                    ts("vector", rb[:], rb[:], stat[:, 12:13], stat[:, 16:17], ALU.subtract, ALU.mult, [rbt, stt_], [rbt])
                    tt("gpsimd", rb[:], rb[:], LG[:], ALU.mult, [rbt, "LG"], [rbt])
                    tt("gpsimd", rb[:], rb[:], LB[:], ALU.add, [rbt, "LB"], [rbt])
                    o = dma("sync", dst_dram[t * 128:(t + 1) * 128, :], rb[:], reads=[rbt], writes=[f"dst{t}"])
                    final_ops.append(o)
                    if also_xt:
                        tb = TB[t % 2]
                        tbt = f"TB{t % 2}"
                        S.op("scalar", lambda e, tb=tb, rb=rb: e.copy(out=tb[:], in_=rb[:]), reads=[rbt], writes=[tbt])
                    if t + 6 < TT:
                        load_res(t + 6)

                def s3b(t):
                    if also_xt:
                        tb = TB[t % 2]
                        tbt = f"TB{t % 2}"
                        pb, pbt = nb()
                        pv = pb[:].bitcast(BF16).rearrange("p (k t) -> p k t", k=8)

                        def fn(e, tb=tb, pv=pv):
                            ins = None
                            for k in range(8):
                                ins = e.transpose(out=pv[:, k, :], in_=tb[:, k * 128:(k + 1) * 128], identity=ident[:])
                            return ins
                        S.op("tensor", fn, reads=[tbt, "ident"], writes=[pbt])
                        S.op("scalar", lambda e, pv=pv, t=t: e.copy(out=XT[:, :, t * 128:(t + 1) * 128], in_=pv), reads=[pbt], writes=[f"XT{t // 4}"])

                for t in range(TT + 3):
                    if t < TT:
                        s1(t)
                    if 0 <= t - 2 < TT:
                        s3(t - 2)
                    if 0 <= t - 3 < TT:
                        s3b(t - 3)


            wload(WIN[:, :, 0:672], w_in0[:, 0:672], ["WIN"])
            wload(WIN[:, :, 672:1312], w_in0[:, 672:1312], ["WINb"])
            wload(WIN[:, :, 1312:1952], w_in0[:, 1312:1952], ["WINc"])
            dma("gpsimd", WKS[:, :, 64:96], w_krs.rearrange("(k p) c -> p k c", p=128), writes=["WKS"])
            WFLAT = WSM[:].rearrange("p k c -> p (k c)")
            WUQ = WFLAT[:, 0:3456].rearrange("p (k c) -> p k c", k=3)
            WUQS = WFLAT[:, 3456:6912].rearrange("p (k c) -> p k c", k=3)
            wload(WUQ, w_uq, ["WSM"])
            wload(WUQS, w_uqs, ["WSMs"])
            mem_kv(0)
            ckpt(2, lambda: {"MK": (MK[:], slice(0, 128), 512, 0), "MVA": (MVA[:, :, 0, :], slice(0, 128), 264, 512)})
            S.barrier(lambda e: e.memset(stat[:, 19:20], 0.0), skip_dma=True)
            for c in range(NB):
                for (nch, col0, ncols, gcol, dst, dtok) in ((3, 0, 384, 0, CQT, "CQT"), (2, 384, 256, 3, LATo, "LATo")):
                    c32 = []
                    sqs = []
                    for i in range(nch):
                        pb, pbt = proj_chunk(c, slice(col0 + i * 128, col0 + (i + 1) * 128))
                        sq = PT[i]
                        if KF >= 2:
                            act(sq[:], pb[:], AF.Square, [pbt], [f"PT{i}"])
                        vcopy("vector", TMP[2 + i][:], pb[:], [pbt], [f"TMP{2 + i}"])
                        sqs.append((sq, f"PT{i}"))
                        c32.append((TMP[2 + i], f"TMP{2 + i}"))
                    pb, pbt = nb()
                    if KF >= 3:
                        mm(pb[:], [(ones_bf[:], sq[:]) for sq, _ in sqs], [t for _, t in sqs] + ["ones_bf"], [pbt])
                        ts("vector", TMP[5][:], pb[:], 1.0 / ncols, EPS, ALU.mult, ALU.add, [pbt], ["TMP5"])
                    if KF >= 4:
                        act(TMP[5][:], TMP[5][:], AF.Ln, ["TMP5"], ["TMP5"])
                        act(TMP[5][:], TMP[5][:], AF.Exp, ["TMP5"], ["TMP5"], scale=-0.5)
                    for i in range(nch if KF >= 5 else 0):
                        t32, t32t = c32[i]
                        stt(dst[:, i, blk(c)], t32[:], smallc[:, gcol + i:gcol + i + 1], TMP[5][:], ALU.mult, ALU.mult,
                            [t32t, "TMP5", "qn", "kvn"], [f"{dtok}{c}"])
                    ckpt(311 if dtok == "CQT" else 312, lambda: {"CQT": (CQT[:, :, 0:512], slice(0, 128), 1536, 0), "LATo": (LATo[:, :, 0:512], slice(0, 128), 1024, 1536)})
                pb, pbt = proj_chunk(c, slice(576, 672))
                pb2, pbt2 = proj_chunk(c, slice(0, 96), w=WKS, wtok="WKS")
                tt("vector", TMP[0][R, :], pb[R, :], CTAB[R, blk(c)], ALU.mult, [pbt, "CTAB"], ["TMP0"])
                tt("vector", TMP[1][R, :], pb2[R, :], STAB[R, blk(c)], ALU.mult, [pbt2, "STAB"], ["TMP1"])
                tt("vector", KRo[R, blk(c)], TMP[0][R, :], TMP[1][R, :], ALU.add, ["TMP0", "TMP1"], ["KRo"])
                ckpt(313, lambda: {"CQT": (CQT[:, :, 0:512], slice(0, 128), 1536, 0), "LATo": (LATo[:, :, 0:512], slice(0, 128), 1024, 1536), "KRo": (KRo[64:96, 0:512], slice(64, 96), 512, 2560)})

            ckpt(3, lambda: {"CQT": (CQT[:, :, 0:512], slice(0, 128), 1536, 0), "LATo": (LATo[:, :, 0:512], slice(0, 128), 1024, 1536),
                     "KRo": (KRo[64:96, 0:512], slice(64, 96), 512, 2560), "G": (G[:, :, 0:512], slice(0, 128), 4096, 3072)})
            for k in range(2):
                dma("sync", lat_b[k].ap(), LATo[:, k, :], reads=[f"LATo{c}" for c in range(NB)], writes=[f"lat_b{k}"])
            dma("sync", lat_b[2].ap(), KRo[R, :], reads=["KRo"], writes=["lat_b2"])
            for i in range(3):
                S.op("gpsimd", lambda e, i=i: e.collective_compute("AllGather", ALU.bypass, replica_groups=groups, ins=[lat_b[i].ap().opt()], outs=[lat_g[i].ap().opt()]),
                     reads=[f"lat_b{i}"], writes=[f"lat_g{i}"], dma="cc")
            for c in range(NB):
                gate_qmem(c, 672, wtoks=["WINb"] * 5 + ["WINc"] * 5)

            ckpt(4)
            S.barrier(lambda e: e.memset(stat[:, 20:21], 0.0), skip_cc=True)
            S.op("gpsimd", lambda e: e.memset(stat[:, 26:27], 0.0), writes=["QTGATE"])
            for jp in range(4):
                for k in range(2):
                    dma("sync", LAT[:, k, :].rearrange("p (c j t) -> p c j t", c=4, j=4)[:, :, jp, :],
                        lat_g[k].ap()[jp * 128:(jp + 1) * 128, :].rearrange("p (c t) -> p c t", c=4),
                        reads=[f"lat_g{k}"], writes=["LAT"])
            qit = 0
            for c in range(NB):
                for h in range(12):
                    hc = slice(h * 96, (h + 1) * 96)
                    ta, tat = TMP[2 * (qit % 2)], f"TMP{2 * (qit % 2)}"
                    tb_, tbt_ = TMP[2 * (qit % 2) + 1], f"TMP{2 * (qit % 2) + 1}"
                    qit += 1
                    pb, pbt = nb()
                    mm(pb[0:96, :], [(WUQ[:, k, hc], CQT[:, k, blk(c)]) for k in range(3)], ["WSM", f"CQT{c}"], [pbt])
                    pb2, pbt2 = nb()
                    mm(pb2[0:96, :], [(WUQS[:, k, hc], CQT[:, k, blk(c)]) for k in range(3)], ["WSMs", f"CQT{c}"], [pbt2])
                    S.op("scalar", lambda e, pb=pb, h=h, c=c: e.copy(out=QT[0:64, h, blk(c)], in_=pb[0:64, :]), reads=[pbt, "QTGATE"], writes=[f"QT{c}"], merge=True)
                    tt("vector", ta[R, :], pb[R, :], CTAB[R, blk(c)], ALU.mult, [pbt, "CTAB"], [tat])
                    tt("vector", tb_[R, :], pb2[R, :], STAB[R, blk(c)], ALU.mult, [pbt2, "STAB"], [tbt_])
                    S.op("gpsimd", lambda e, h=h, c=c, ta=ta, tb_=tb_: e.tensor_tensor(out=QT[R, h, blk(c)], in0=ta[R, :], in1=tb_[R, :], op=ALU.add),
                         reads=[tat, tbt_, "QTGATE"], writes=[f"QT{c}"], merge=True)


            ckpt(5, lambda: {"QT": (QT[0:96, 0:4, 0:512], slice(0, 96), 2048, 0), "G": (G[:, :, 0:512], slice(0, 128), 4096, 3072)})
            S.barrier(lambda e: e.memset(stat[:, 21:22], 0.0))
            for i in range(2):
                S.op("gpsimd", lambda e, i=i: e.memset(VA[i][:, :, 64:66], 1.0), writes=[f"VA{i}"])
            for jp in range(4):
                for i in range(1):
                    dma("sync", KT[i][64:96, :].rearrange("p (c j t) -> p c j t", c=4, j=4)[:, :, jp, :],
                        lat_g[2].ap()[jp * 32:(jp + 1) * 32, :].rearrange("p (c t) -> p c t", c=4),
                        reads=["lat_g2"], writes=[f"KTr{i}"])
            WUKV = WSM[:].rearrange("p k c -> p (k c)")[:, 0:3072].rearrange("p (k c) -> p k c", k=2)
            wload(WUKV, w_ukv, ["WSM", "WSMs"])

            def wukv(k, col):
                return WUKV[:, k, col], "WSM"

            ckpt(6, lambda: {"LAT": (LAT[:, 0, 0:4096], slice(0, 128), 4096, 0), "KT": (KT[0][64:96, 0:4096], slice(64, 96), 4096, 4096)})
            deferred_fin = []
            for h in range(NHEADS_DBG):
                b_ = h % 2
                for kb in range(16):
                    pb, pbt = nb((5, 6))
                    wc = slice(h * 128, h * 128 + 64)
                    mm(pb[0:64, :], [(wukv(k, wc)[0], LAT[:, k, blk(kb)]) for k in range(2)], [wukv(0, wc)[1], "LAT"], [pbt])
                    if kb % 2 == 0:
                        vcopy("vector", KT[0][0:64, blk(kb)], pb[0:64, :], [pbt], [f"KT0_{kb}"])
                    else:
                        S.op("scalar", lambda e, pb=pb, b_=b_, kb=kb: e.copy(out=KT[0][0:64, blk(kb)], in_=pb[0:64, :]), reads=[pbt], writes=[f"KT0_{kb}"])
                for g8 in range(8):
                    pb, pbt = nb((5, 6))
                    wc = slice(h * 128 + 64, h * 128 + 128)

                    def fn(e, pb=pb, g8=g8, wc=wc):
                        ins = None
                        for i in range(8):
                            kt = g8 * 8 + i
                            for k in range(2):
                                ins = e.matmul(pb[:, i * 64:(i + 1) * 64], lhsT=LAT[:, k, kt * 128:(kt + 1) * 128], rhs=wukv(k, wc)[0], start=(k == 0), stop=(k == 1))
                        return ins
                    S.op("tensor", fn, reads=["LAT", wukv(0, wc)[1]], writes=[pbt])
                    vcopy("vector", VA[b_][:, g8 * 8:(g8 + 1) * 8, 0:64], pb[:].rearrange("p (t d) -> p t d", t=8), [pbt, f"VA{b_}"], [f"VA{b_}_{g8}"])
                if h == NHEADS_DBG - 1:
                    wload(WSM[:], w_out[0], ["WSM", "WSMs"])
                for c in range(NB):
                    nkb = 4 * c + 4
                    tiles = []
                    for kb in range(nkb):
                        for q4 in range(4):
                            kt = kb * 4 + q4
                            mk = MSK[:, (kb - 4 * c) * 4 + q4, :] if kb >= 4 * c else None
                            tiles.append((KT[0][0:96, kt * 128:(kt + 1) * 128], [f"KT0_{kb}", "KTr0"], VA[b_][:, kt, :], f"VA{b_}_{kt // 8}", mk))
                    ob = banks[3 + (h * NB + c) % 2]
                    obt = f"bank{3 + (h * NB + c) % 2}"
                    attend(tiles, QT[0:96, h, blk(c)], f"QT{c}", SC_ATT, ob, obt, 96,
                           lambda ob=ob, obt=obt, h=h, c=c: finalize_head(ob, obt, h // 2, h % 2, c),
                           sb_list=(0, 1, 2, 7), npt=6, LA=3, defer=deferred_fin)
            while deferred_fin:
                deferred_fin.pop(0)()

            ckpt(7, lambda: {"G": (G[:, :, 0:512], slice(0, 128), 4096, 0), "G3": (G[:, :, 1536:2048], slice(0, 128), 4096, 4096)})
            S.barrier(lambda e: e.memset(stat[:, 23:24], 0.0))
            out_ln(0, x, hs.ap(), True, w_prefetched=True)
            S.barrier(lambda e: e.memset(stat[:, 24:25], 0.0))
            ckpt(8)
            final_ops.clear()

            WU1 = WSM[:].rearrange("p k c -> p (k c)")[:, 0:6144].rearrange("p (k c) -> p k c", k=8)
            wload(WU1, w_in1[:, 0:768], ["WSM", "WSMs"])
            mem_kv(1)
            wload(WG1, w_in1[:, 768:2048], ["WIN", "WMEM", "MT", "TB0", "TB1"])
            dma("sync", L1C[:, 0:48], l1pack[:, :], writes=["L1C"])
            dma("gpsimd", WR[:], w_rbd[:, :, :], writes=["WR"])
            dma("gpsimd", WI[:], w_ibd[:, :, :], writes=["WI"])
            act(L1C[:, 48:54], L1C[:, 42:48], AF.Exp, ["L1C"], ["L1C"], scale=-1.0)
            act(L1C[:, 48:54], L1C[:, 48:54], AF.Ln, ["L1C"], ["L1C"], bias=1.0, scale=1.0)
            ts("vector", L1C[:, 54:60], L1C[:, 48:54], -16.0, None, ALU.mult, None, ["L1C"], ["L1C"])
            ts("vector", L1C[:, 48:54], L1C[:, 48:54], -8.0, None, ALU.mult, None, ["L1C"], ["L1C"])

            UT = sb("UT", [128, 6, 16], F32)
            S.op("gpsimd", lambda e: e.memset(UT[:], 0.0), writes=["UT"])
            for c in range(NB):
                for ch in range(6):
                    pb, pbt = nb()
                    mm(pb[:, 0:3], [(WU1[:, k, ch * 128:(ch + 1) * 128], XT[:, k, c * 512 + 509:c * 512 + 512]) for k in range(8)], ["WSM", f"XT{c}"], [pbt])
                    vcopy("vector", UT[:, ch, c * 4:c * 4 + 3], pb[:, 0:3], [pbt], ["UT"])
            dma("sync", uh_b.ap().rearrange("(k p) t -> p k t", p=128), UT[:], reads=["UT"], writes=["uh_b"])
            S.op("gpsimd", lambda e: e.collective_compute("AllGather", ALU.bypass, replica_groups=groups, ins=[uh_b.ap().opt()], outs=[uh_g.ap().opt()]),
                 reads=["uh_b"], writes=["uh_g"], dma="cc")
            HALL = sb("HALL", [128, 6, 4, 16], F32)
            for jp in range(4):
                dma("sync", HALL[:, :, jp, :], uh_g.ap()[jp * 768:(jp + 1) * 768, :].rearrange("(k p) t -> p k t", p=128), reads=["uh_g"], writes=["HALL"])
            S.op("gpsimd", lambda e: e.memset(HALO[:], 0.0), writes=["HALO"])
            for c in range(NB):
                for jj in range(4):
                    if jj == 0 and c == 0:
                        continue
                    src = HALL[:, :, jj - 1, c * 4:c * 4 + 3] if jj >= 1 else HALL[:, :, 3, (c - 1) * 4:(c - 1) * 4 + 3]
                    stt(HALO[:, :, c * 4:c * 4 + 3], src, SEL[:, jj:jj + 1], HALO[:, :, c * 4:c * 4 + 3], ALU.mult, ALU.add, ["HALL", "SEL", "HALO"], ["HALO"])

            ckpt(9)
            TMPB = [view(A4, 10240 + i * 2048, 2048, F32) for i in range(4)] + \
                   [MSK[:].rearrange("p a b -> p (a b)")[:, i * 2048:(i + 1) * 2048].bitcast(F32) for i in range(4)]
            ZER = TMPB[6]
            S.op("gpsimd", lambda e: e.memset(ZER[:], 0.0), writes=["ZER"])
            def ctx(it):
                c, ch = it // 6, it % 6

                def T(k):
                    return (TMP[k], f"TMP{k}") if it % 2 == 0 else (TMPB[k], f"TMPB{k}")
                return c, ch, T, (UX[it % 2], f"UX{it % 2}"), (XCB[it % 2], f"XCB{it % 2}")

            gate_ps = {}

            def stA(it):
                c, ch, T, (ux, uxt), (xcb, xcbt) = ctx(it)
                pb, pbt = proj_chunk(c, slice(ch * 128, (ch + 1) * 128), w=WU1, wtok="WSM")
                S.op("scalar", lambda e, ux=ux, pb=pb: e.copy(out=ux[:, 3:515], in_=pb[:]), reads=[pbt], writes=[uxt])
                vcopy("vector", ux[:, 0:3], HALO[:, ch, c * 4:c * 4 + 3], ["HALO", uxt], [uxt])
                xc, xct = T(0)
                cw = lambda tap: L1C[:, ch * 4 + tap:ch * 4 + tap + 1]
                ts("vector", xc[:], ux[:, 0:512], cw(0), L1C[:, 24 + ch:25 + ch], ALU.mult, ALU.add, [uxt, "L1C"], [xct])
                for tap in range(1, 4):
                    stt(xc[:], ux[:, tap:tap + 512], cw(tap), xc[:], ALU.mult, ALU.add, [uxt, "L1C", xct], [xct])
                vcopy("vector", xcb[:], xc[:], [xct], [xcbt])
                pr, prt = nb()
                mm(pr[:], [(WR[:, ch, :], xcb[:])], ["WR", xcbt], [prt])
                pi_, pit = nb()
                mm(pi_[:], [(WI[:, ch, :], xcb[:])], ["WI", xcbt], [pit])
                gate_ps[it] = (pr, prt, pi_, pit)

            def stB(it):
                c, ch, T, _, _ = ctx(it)
                pr, prt, pi_, pit = gate_ps[it]
                r_, rt = T(1)
                i_, itk = T(2)
                a_, at_ = T(3)
                m_, mt_ = T(4)
                act(r_[:], pr[:], AF.Sigmoid, [prt, "L1C"], [rt], bias=L1C[:, 30 + ch:31 + ch])
                act(i_[:], pi_[:], AF.Sigmoid, [pit, "L1C"], [itk], bias=L1C[:, 36 + ch:37 + ch])
                act(a_[:], r_[:], AF.Exp, [rt, "L1C"], [at_], scale=L1C[:, 48 + ch:49 + ch])
                act(m_[:], r_[:], AF.Exp, [rt, "L1C"], [mt_], scale=L1C[:, 54 + ch:55 + ch])
                act(m_[:], m_[:], AF.Sqrt, [mt_], [mt_], scale=-1.0, bias=1.0)

            def stC(it):
                c, ch, T, _, _ = ctx(it)
                xc, xct = T(0)
                i_, itk = T(2)
                a_, at_ = T(3)
                m_, mt_ = T(4)
                tt("gpsimd", i_[:], i_[:], xc[:], ALU.mult, [itk, xct], [itk])
                tt("gpsimd", i_[:], i_[:], m_[:], ALU.mult, [itk, mt_], [itk])
                hl, hlt = T(5)
                pp, ppt = T(1)
                S.op("vector", lambda e, hl=hl, a_=a_, i_=i_: e.tensor_tensor_scan(out=hl[:], data0=a_[:], data1=i_[:], initial=0.0, op0=ALU.mult, op1=ALU.add), reads=[at_, itk], writes=[hlt])
                S.op("vector", lambda e, pp=pp, a_=a_: e.tensor_tensor_scan(out=pp[:], data0=a_[:], data1=ZER[:], initial=1.0, op0=ALU.mult, op1=ALU.add), reads=[at_, "ZER"], writes=[ppt])
                vcopy("gpsimd", HL[:, ch, blk(c)], hl[:], [hlt], [f"HL{c}"])
                vcopy("gpsimd", PP[:, ch, blk(c)], pp[:], [ppt], [f"PP{c}"])
                vcopy("vector", SUMS[:, ch, c * 2:c * 2 + 1], pp[:, 511:512], [ppt], ["SUMS"])
                vcopy("vector", SUMS[:, ch, c * 2 + 1:c * 2 + 2], hl[:, 511:512], [hlt], ["SUMS"])

            NIT = NB * 6
            stA(0)
            for it in range(NIT):
                if it + 1 < NIT:
                    stA(it + 1)
                stB(it)
                stC(it)
            dma("sync", sm_b.ap().rearrange("(k p) t -> p k t", p=128), SUMS[:, :, 0:8], reads=["SUMS"], writes=["sm_b"])
            S.op("gpsimd", lambda e: e.collective_compute("AllGather", ALU.bypass, replica_groups=groups, ins=[sm_b.ap().opt()], outs=[sm_g.ap().opt()]),
                 reads=["sm_b"], writes=["sm_g"], dma="cc")
            ckpt(10)
            wload(WSM[:], w_out[1], ["WSM", "WSMs"])
            SALL = sb("SALL", [128, 6, 4, 8], F32)
            STT = sb("STT", [128, 6, 17], F32)
            HIN = sb("HIN", [128, 6, 4], F32)

            def carry_states():
                for jp in range(4):
                    dma("sync", SALL[:, :, jp, :], sm_g.ap()[jp * 768:(jp + 1) * 768, :].rearrange("(k p) t -> p k t", p=128), reads=["sm_g"], writes=["SALL"])
                S.op("gpsimd", lambda e: e.memset(STT[:], 0.0), writes=["STT"])
                for g in range(16):
                    cc_, jj = g // 4, g % 4
                    tt("vector", STT[:, :, g + 1:g + 2], SALL[:, :, jj, cc_ * 2:cc_ * 2 + 1], STT[:, :, g:g + 1], ALU.mult, ["SALL", "STT"], ["STT"])
                    tt("vector", STT[:, :, g + 1:g + 2], STT[:, :, g + 1:g + 2], SALL[:, :, jj, cc_ * 2 + 1:cc_ * 2 + 2], ALU.add, ["SALL", "STT"], ["STT"])
                S.op("gpsimd", lambda e: e.memset(HIN[:], 0.0), writes=["HIN"])
                for c in range(NB):
                    for jj in range(4):
                        stt(HIN[:, :, c:c + 1], STT[:, :, 4 * c + jj:4 * c + jj + 1], SEL[:, jj:jj + 1], HIN[:, :, c:c + 1], ALU.mult, ALU.add, ["STT", "SEL", "HIN"], ["HIN"])

            def fixup(c):
                for ch in range(6):
                    t0, t0t = TMP[(c * 6 + ch) % 2], f"TMP{(c * 6 + ch) % 2}"
                    stt(t0[:], PP[:, ch, blk(c)], HIN[:, ch, c:c + 1], HL[:, ch, blk(c)], ALU.mult, ALU.add, [f"PP{c}", f"HL{c}", "HIN"], [t0t])
                    tt("gpsimd", G[:, ch, blk(c)], t0[:], G[:, ch, blk(c)], ALU.mult, [t0t, f"G{ch}_{c}"], [f"G{ch}_{c}"])

            for c in range(NB):
                gate_qmem(c, 0, w=WG1, wtok="WIN")
                if c == 0:
                    carry_states()
                fixup(c)
            ckpt(11)
            S.barrier(lambda e: e.memset(stat[:, 25:26], 0.0))
            out_ln(1, hs.ap(), out, False, w_prefetched=True)
        except _Stop:
            pass
        if not final_ops:
            for name, (ap, rows, ncols, off) in (DUMPS.items() if not os.environ.get('KNODUMP') else []):
                dst = dbgout[rows, off:off + ncols]
                if len(ap.shape) == 3:
                    dst = dst.rearrange("p (k t) -> p k t", k=ap.shape[1])
                final_ops.append(dma("gpsimd", dst, ap, reads=list(S.last_w.keys())))
            final_ops.append(dma("sync", out[0:128, :], x[0:128, :]))
        S.emit(final_wait_ops=list(final_ops))
    return nc


_NC_CACHE = {}


def _prep_inputs(inp):
    f32 = np.float32
    x = np.asarray(inp["x"], f32)
    mem = np.asarray(inp["mem"], f32)
    pos = np.asarray(inp["positions"], np.int32)
    w_in0 = np.ascontiguousarray(np.asarray(inp["mla_w_in"], f32)[0])
    kr = w_in0[:, 640:672]
    w_krs = np.ascontiguousarray(np.concatenate([kr[:, 16:32], kr[:, 0:16]], axis=1))
    w_uq = np.ascontiguousarray(np.asarray(inp["mla_w_uq"], f32)[0])
    wq3 = w_uq.reshape(384, 12, 96)
    w_uqs = np.ascontiguousarray(np.concatenate([wq3[:, :, 0:64], wq3[:, :, 80:96], wq3[:, :, 64:80]], axis=2).reshape(384, 1152))
    qn = np.ascontiguousarray(np.asarray(inp["mla_q_norm"], f32)[0].reshape(3, 128).T)
    kvn = np.ascontiguousarray(np.asarray(inp["mla_kv_norm"], f32)[0].reshape(2, 128).T)
    w_ukv = np.ascontiguousarray(np.asarray(inp["mla_w_ukv"], f32)[0])
    w_in1 = np.ascontiguousarray(np.asarray(inp["lru_w_in"], f32)[0])
    cw = np.asarray(inp["lru_conv_w"], f32)[0]
    conv_w = np.ascontiguousarray(cw.reshape(4, 6, 128).transpose(2, 1, 0))
    pp = lambda v: np.ascontiguousarray(np.asarray(v, f32)[0].reshape(6, 128).T)
    conv_b = pp(inp["lru_conv_b"])
    b_r = pp(inp["lru_b_rgate"])
    b_i = pp(inp["lru_b_igate"])
    lam = pp(inp["lru_lambda"])

    def blockdiag(w):
        w = np.asarray(w, f32)[0]
        o = np.zeros((128, 6, 128), f32)
        for g in range(12):
            ch, e = g // 2, g % 2
            o[e * 64:(e + 1) * 64, ch, e * 64:(e + 1) * 64] = w[g]
        return o
    w_rbd = blockdiag(inp["lru_w_rgate"])
    w_ibd = blockdiag(inp["lru_w_igate"])
    w_mem = np.ascontiguousarray(np.asarray(inp["w_mem_kv"], f32))
    w_out = np.ascontiguousarray(np.asarray(inp["w_out"], f32))
    ln_g = np.ascontiguousarray(np.asarray(inp["ln_g"], f32))
    ln_b = np.ascontiguousarray(np.asarray(inp["ln_b"], f32))
    invf16 = (np.float32(10000.0) ** (-np.arange(16, dtype=np.float32) / np.float32(16))).astype(f32)
    invf = np.concatenate([invf16, invf16]).reshape(32, 1).astype(f32)
    sgn = np.concatenate([-np.ones(16, f32), np.ones(16, f32)]).reshape(32, 1)
    l1pack = np.ascontiguousarray(np.concatenate([conv_w.reshape(128, 24), conv_b, b_r, b_i, lam], axis=1)).astype(f32)
    shared = dict(w_in0=w_in0, w_krs=w_krs, w_uq=w_uq, w_uqs=w_uqs, w_ukv=w_ukv, w_in1=w_in1,
                  w_rbd=w_rbd, w_ibd=w_ibd, l1pack=l1pack,
                  w_mem=w_mem, w_out=w_out, ln_g=ln_g, ln_b=ln_b)
    maps = []
    kk = np.arange(128)[:, None]
    qq = np.arange(512)[None, :]
    for r in range(8):
        b, j = r // 4, r % 4
        tok = np.concatenate([np.arange((4 * c + j) * 512, (4 * c + j + 1) * 512) for c in range(4)])
        m = dict(shared)
        m["x"] = np.ascontiguousarray(x[b, tok])
        m["posb"] = np.ascontiguousarray(np.broadcast_to(pos[b, tok][None, :], (32, NT))).astype(np.int32)
        m["mem"] = np.ascontiguousarray(mem[b])
        msk = np.zeros((128, 16, 512), np.uint8)
        for g in range(4):
            for q4 in range(4):
                if g < j:
                    msk[:, g * 4 + q4, :] = 1
                elif g == j:
                    msk[:, g * 4 + q4, :] = ((q4 * 128 + kk) <= qq).astype(np.uint8)
        m["masks"] = msk
        cp = np.zeros((128, 64), f32)
        cp[:, 0:3] = qn
        cp[:, 3:5] = kvn
        cp[64:96, 5:6] = invf
        cp[64:96, 6:7] = sgn
        cp[:, 8 + j] = 1.0
        m["cpack"] = cp
        maps.append(m)
    return maps


def kernel(**inputs):
    if "nc" not in _NC_CACHE:
        _NC_CACHE["nc"] = build()
    nc = _NC_CACHE["nc"]
    maps = _prep_inputs(inputs)
    res = run_bass_kernel_spmd(nc, maps, core_ids=list(range(8)))
    outp = np.zeros((2, SEQ, 1024), np.float32)
    for r in range(8):
        b, j = r // 4, r % 4
        o = np.asarray(res.results[r]["out"], np.float32)
        for c in range(4):
            outp[b, (4 * c + j) * 512:(4 * c + j + 1) * 512] = o[c * 512:(c + 1) * 512]
    return outp
```

```python
import contextlib
import os
import math
import numpy as np
import concourse.bass as bass
import concourse.mybir as mybir
from concourse.bass_utils import run_bass_kernel_spmd

F32 = mybir.dt.float32
BF16 = mybir.dt.bfloat16
U8 = mybir.dt.uint8
I32 = mybir.dt.int32
AF = mybir.ActivationFunctionType
ALU = mybir.AluOpType

PHASE = 3000
COMPUTE = ("tensor", "vector", "scalar", "gpsimd")


class Op:
    __slots__ = ("eng", "fn", "deps", "idx", "is_dma", "id")


class Sched:
    def __init__(self, nc, ndma_sems=16):
        self.nc = nc
        self.ops = []
        self.per_eng = {e: [] for e in ("tensor", "vector", "scalar", "gpsimd", "sync")}
        self.last_w = {}
        self.readers = {}
        self.ndma_sems = ndma_sems
        self.dma_count = {}
        self.ccount = {}
        self.cc_ops = []
        self.barrier_dep = None

    def op(self, eng, fn, reads=(), writes=(), dma=False, extra_deps=(), merge=False):
        o = Op()
        o.eng = eng
        o.fn = fn
        o.is_dma = dma
        o.id = len(self.ops)
        deps = set(extra_deps)
        if self.barrier_dep is not None:
            deps.add(self.barrier_dep)
        for t in list(reads) + ([] if merge else list(writes)):
            for w in self.last_w.get(t, ()):
                deps.add(w)
        for t in writes:
            for r in self.readers.get(t, {}).values():
                deps.add(r)
        for t in reads:
            if t.startswith("bank"):
                for r in self.readers.get(t, {}).values():
                    deps.add(r)
        o.deps = sorted(deps)
        for t in writes:
            if merge:
                self.last_w[t] = set(self.last_w.get(t, ())) | {o.id}
            else:
                self.last_w[t] = {o.id}
                self.readers[t] = {}
        for t in reads:
            key = (eng,) if not dma else (eng, o.id)
            self.readers.setdefault(t, {})[key] = o.id
        if dma == "cc":
            o.idx = len(self.cc_ops)
            self.cc_ops.append(o)
        elif dma:
            n = self.dma_count.get(eng, 0)
            self.dma_count[eng] = n + 1
            o.idx = n
        else:
            o.idx = self.ccount.get(eng, 0)
            self.ccount[eng] = o.idx + 1
        self.per_eng[eng].append(o)
        self.ops.append(o)
        return o.id

    def barrier(self, mk, skip_cc=False, skip_dma=False):
        deps = set()
        for e, lst in self.per_eng.items():
            last_c = None
            for o in lst:
                if o.is_dma == "cc" and skip_cc:
                    continue
                if o.is_dma and skip_dma:
                    continue
                if o.is_dma:
                    deps.add(o.id)
                else:
                    last_c = o.id
            if last_c is not None:
                deps.add(last_c)
        b = self.op("gpsimd", mk, extra_deps=deps)
        self.barrier_dep = b
        return b

    def emit(self, final_wait_ops=()):
        nc = self.nc
        with contextlib.ExitStack() as st:
            csem = {}
            for e in COMPUTE:
                n = self.ccount.get(e, 0)
                nph = max(1, (n + PHASE - 1) // PHASE)
                csem[e] = [st.enter_context(nc.semaphore(f"c_{e}_{i}")) for i in range(nph)]
            dsem = {}
            for e, n in self.dma_count.items():
                k = min(self.ndma_sems, n)
                dsem[e] = [st.enter_context(nc.semaphore(f"d_{e}_{i}")) for i in range(k)]
            ccsem = [st.enter_context(nc.semaphore(f"cc_{i}")) for i in range(len(self.cc_ops))]
            block = st.enter_context(nc.Block())
            ops = self.ops

            def target(o):
                if o.is_dma == "cc":
                    return ccsem[o.idx], 1
                if o.is_dma:
                    k = len(dsem[o.eng])
                    return dsem[o.eng][o.idx % k], 16 * (o.idx // k + 1)
                return csem[o.eng][o.idx // PHASE], (o.idx % PHASE) + 1

            def run_engine(ename, eng):
                waited = {}
                waited_dma = set()
                for o in self.per_eng[ename]:
                    for d in o.deps:
                        p = ops[d]
                        if p.is_dma:
                            if d in waited_dma:
                                continue
                            waited_dma.add(d)
                            s, v = target(p)
                            eng.wait_ge(s, v)
                        else:
                            if p.eng == ename and ename == "tensor":
                                continue
                            if waited.get(p.eng, -1) >= p.idx:
                                continue
                            waited[p.eng] = p.idx
                            s, v = target(p)
                            eng.wait_ge(s, v)
                    if o.is_dma == "cc":
                        s, v = target(o)
                        o.fn(eng).then_inc(s)
                    elif o.is_dma:
                        k = len(dsem[o.eng])
                        if o.idx >= k:
                            eng.wait_ge(dsem[o.eng][o.idx % k], 16 * (o.idx // k))
                        s, v = target(o)
                        o.fn(eng).then_inc(s, 16)
                    else:
                        s, v = target(o)
                        o.fn(eng).then_inc(s, 1)
                if ename == "sync":
                    for fo in final_wait_ops:
                        s, v = target(ops[fo])
                        eng.wait_ge(s, v)

            @block.sync
            def _(e):
                run_engine("sync", e)

            @block.tensor
            def _(e):
                run_engine("tensor", e)

            @block.vector
            def _(e):
                run_engine("vector", e)

            @block.scalar
            def _(e):
                run_engine("scalar", e)

            @block.gpsimd
            def _(e):
                run_engine("gpsimd", e)


NT = 2048
NB = 4
TT = 16
SEQ = 8192
ALPHA = 4.0 ** 0.25
EPS = 1e-6
SC_ATT = 1.0 / math.sqrt(96.0)
SC_MEM = 1.0 / 8.0
TWO_PI = float(2.0 * np.pi)
DEBUG = False
NHEADS_DBG = int(os.environ.get("KHEADS", "12"))
KF = int(os.environ.get("KF", "9"))


class _Stop(Exception):
    pass


STAGE = int(os.environ.get("KSTAGE", "99"))


def build():
    nc = bass.Bass("TRN2", target_bir_lowering=False)
    DUMPS = {}

    def ckpt(n, dumps=None):
        if STAGE == n:
            DUMPS.update(dumps() if dumps else {})
            raise _Stop()

    def din(name, shape, dt=F32):
        return nc.dram_tensor(name, list(shape), dt, kind="ExternalInput").ap()

    x = din("x", [NT, 1024])
    posb = din("posb", [32, NT], I32)
    cpack = din("cpack", [128, 64])
    l1pack = din("l1pack", [128, 48])
    mem = din("mem", [256, 1024])
    masks = din("masks", [128, 16, 512], U8)
    w_in0 = din("w_in0", [1024, 1952])
    w_krs = din("w_krs", [1024, 32])
    w_uq = din("w_uq", [384, 1152])
    w_uqs = din("w_uqs", [384, 1152])
    w_ukv = din("w_ukv", [256, 1536])
    w_in1 = din("w_in1", [1024, 2048])
    w_rbd = din("w_rbd", [128, 6, 128])
    w_ibd = din("w_ibd", [128, 6, 128])
    w_mem = din("w_mem", [2, 1024, 512])
    w_out = din("w_out", [2, 1024, 1024])
    ln_g = din("ln_g", [2, 1024])
    ln_b = din("ln_b", [2, 1024])
    out = nc.dram_tensor("out", [NT, 1024], F32, kind="ExternalOutput").ap()
    dbgout = nc.dram_tensor("dbgout", [128, 8192], F32, kind="ExternalOutput").ap() if STAGE != 99 else None

    lat_b = [nc.dram_tensor(f"lat_b{i}", [n, NT], BF16) for i, n in enumerate((128, 128, 32))]
    lat_g = [nc.dram_tensor(f"lat_g{i}", [4 * n, NT], BF16) for i, n in enumerate((128, 128, 32))]
    hs = nc.dram_tensor("hs", [NT, 1024], F32)
    uh_b = nc.dram_tensor("uh_b", [768, 16], F32)
    uh_g = nc.dram_tensor("uh_g", [4 * 768, 16], F32)
    sm_b = nc.dram_tensor("sm_b", [768, 8], F32)
    sm_g = nc.dram_tensor("sm_g", [4 * 768, 8], F32)
    groups = [[0, 1, 2, 3], [4, 5, 6, 7]]

    with contextlib.ExitStack() as st:
        def sb(name, shape, dt):
            return st.enter_context(nc.sbuf_tensor(name, list(shape), dt))

        banks = [st.enter_context(nc.psum_tensor(f"pb{i}", [128, 512], F32)) for i in range(8)]
        S = Sched(nc)

        A1 = sb("A1", [128, 32 * 1024], U8)
        A2 = sb("A2", [128, 24 * 1024], U8)
        A3 = sb("A3", [128, 48 * 1024], U8)
        A4 = sb("A4", [128, 20 * 1024], U8)

        def view(arena, off, nbytes, dt, pat=None, **kw):
            v = arena[:, off:off + nbytes].bitcast(dt)
            if pat:
                v = v.rearrange(pat, **kw)
            return v

        XT = view(A1, 0, 32768, BF16, "p (k t) -> p k t", k=8)
        LAT = view(A1, 0, 32768, BF16, "p (k t) -> p k t", k=2)
        WIN = view(A3, 0, 32768, BF16, "p (k c) -> p k c", k=8)
        WG1 = view(A3, 24576, 20480, BF16, "p (k c) -> p k c", k=8)
        CTAB = view(A2, 0, 8192, F32)
        STAB = view(A2, 8192, 8192, F32)
        PI32 = view(A4, 0, 8192, I32)
        CT2 = view(A4, 8192, 8192, F32)
        KRo = view(A2, 16384, 4096, BF16)
        WMEM = view(A3, 32768, 8192, BF16, "p (k c) -> p k c", k=8)
        MT = view(A3, 40960, 4096, BF16, "p (k c) -> p k c", k=8)
        TB = [view(A3, 45056 + i * 2048, 2048, BF16) for i in range(2)]
        RB = [view(A3, i * 4096, 4096, F32) for i in range(6)]
        LG = view(A3, 32768, 4096, F32)
        LB = view(A3, 36864, 4096, F32)
        UX = [view(A4, i * 2064, 2064, F32) for i in range(2)]
        XCB = [view(A4, 4128 + i * 1024, 1024, BF16) for i in range(2)]
        WR = view(A4, 6176, 1536, BF16, "p (k c) -> p k c", k=6)
        WI = view(A4, 7712, 1536, BF16, "p (k c) -> p k c", k=6)
        KT = [view(A2, 0, 16384, BF16)] * 2
        HL = view(A2, 0, 12288 * 2, BF16, "p (k t) -> p k t", k=6)
        PP = view(A3, 0, 12288 * 2, BF16, "p (k t) -> p k t", k=6)
        QT = view(A3, 0, 49152, BF16, "p (h t) -> p h t", h=12)
        CQT = view(A4, 0, 12288, BF16, "p (k t) -> p k t", k=3)
        LATo = view(A4, 12288, 8192, BF16, "p (k t) -> p k t", k=2)
        VA = [view(A4, i * 8448, 64 * 66 * 2, BF16, "p (t c) -> p t c", c=66) for i in range(2)]

        G = sb("G", [128, 8, NT], BF16)
        QMB = [sb("QMB0", [128, 2, 512], BF16)] * 2
        WSM = sb("WSM", [128, 8, 1024], BF16)
        WKS = sb("WKS", [128, 8, 96], BF16)
        MSK = sb("MSK", [128, 16, 512], U8)
        PT = [sb(f"PT{i}", [128, 512], BF16) for i in range(4)]
        ident = sb("ident", [128, 128], BF16)
        identf = sb("identf", [128, 128], F32)
        ones_bf = sb("ones_bf", [128, 128], BF16)
        ones_f = sb("ones_f", [128, 128], F32)
        smallc = sb("smallc", [128, 64], F32)
        MK = sb("MK", [128, 2, 256], BF16)
        MVA = sb("MVA", [128, 4, 2, 66], BF16)
        TMP = [sb(f"TMP{i}", [128, 512], F32) for i in range(6)]
        RD = TMP[4]
        stat = sb("stat", [128, 32], F32)
        STATS = [stat[:, 0:17], smallc[:, 16:33], smallc[:, 33:50]]
        L1C = sb("L1C", [128, 64], F32)
        HALO = sb("HALO", [128, 6, 16], F32)
        SUMS = sb("SUMS", [128, 6, 8], F32)
        dbg = None

        bank_rr = [0]

        def nb(lst=(0, 1, 2, 3, 4, 5, 6, 7)):
            i = lst[bank_rr[0] % len(lst)]
            bank_rr[0] += 1
            return banks[i], f"bank{i}"

        def mm(out_ap, pairs, reads, writes):
            def fn(e):
                n = len(pairs)
                ins = None
                for i, (l, r) in enumerate(pairs):
                    ins = e.matmul(out_ap, lhsT=l, rhs=r, start=(i == 0), stop=(i == n - 1))
                return ins
            return S.op("tensor", fn, reads=reads, writes=writes)

        def dma(eng, out_ap, in_ap, reads=(), writes=(), extra_deps=(), merge=False, **kw):
            return S.op(eng, lambda e: e.dma_start(out=out_ap, in_=in_ap, **kw), reads=reads, writes=writes, dma=True,
                        extra_deps=extra_deps, merge=merge)

        def act(out_ap, in_ap, func, reads, writes, **kw):
            return S.op("scalar", lambda e: e.activation(out=out_ap, in_=in_ap, func=func, **kw), reads=reads, writes=writes)

        def vcopy(eng, out_ap, in_ap, reads, writes):
            return S.op(eng, lambda e: e.tensor_copy(out=out_ap, in_=in_ap), reads=reads, writes=writes)

        def tt(eng, out_ap, a, b, op, reads, writes):
            return S.op(eng, lambda e: e.tensor_tensor(out=out_ap, in0=a, in1=b, op=op), reads=reads, writes=writes)

        def ts(eng, out_ap, a, s1, s2, op0, op1, reads, writes):
            if op1 is None:
                return S.op(eng, lambda e: e.tensor_scalar(out=out_ap, in0=a, scalar1=s1, scalar2=None, op0=op0), reads=reads, writes=writes)
            return S.op(eng, lambda e: e.tensor_scalar(out=out_ap, in0=a, scalar1=s1, scalar2=s2, op0=op0, op1=op1), reads=reads, writes=writes)

        def stt(out_ap, a, s, b, op0, op1, reads, writes):
            return S.op("vector", lambda e: e.scalar_tensor_tensor(out=out_ap, in0=a, scalar=s, in1=b, op0=op0, op1=op1), reads=reads, writes=writes)

        def blk(c):
            return slice(c * 512, (c + 1) * 512)

        def wload(dst3, src2, writes):
            K, C = dst3.shape[1], dst3.shape[2]
            first = None
            for k in range(K):
                for c0 in range(0, C, 1024):
                    c1 = min(C, c0 + 1024)
                    if first is None:
                        first = dma("gpsimd", dst3[:, k, c0:c1], src2[k * 128:(k + 1) * 128, c0:c1], writes=writes)
                    else:
                        dma("gpsimd", dst3[:, k, c0:c1], src2[k * 128:(k + 1) * 128, c0:c1], writes=writes,
                            extra_deps=S.ops[first].deps, merge=True)

        S.op("gpsimd", lambda e: e.memset(identf[:], 1.0), writes=["identf"])
        S.op("gpsimd", lambda e: e.affine_select(out=identf[:], in_=identf[:], pattern=[[-1, 128]], compare_op=ALU.is_equal, fill=0.0, base=0, channel_multiplier=1), writes=["identf"])
        vcopy("vector", ident[:], identf[:], ["identf"], ["ident"])
        S.op("gpsimd", lambda e: e.memset(ones_bf[:], 1.0), writes=["ones_bf"])
        S.op("gpsimd", lambda e: e.memset(ones_f[:], 1.0), writes=["ones_f"])
        S.op("gpsimd", lambda e: e.memset(WKS[:], 0.0), writes=["WKS"])
        S.op("gpsimd", lambda e: e.memset(MVA[:, :, :, 64:66], 1.0), writes=["MVAones"])
        dma("sync", MSK[:], masks[:, :, :], writes=["MSK"])
        dma("sync", smallc[:], cpack[:, :], writes=["qn", "kvn", "invf", "sgn", "SEL"])
        SEL = smallc[:, 8:12]

        dma("sync", PI32[64:96, :], posb[:, :], writes=["PI32"])
        R = slice(64, 96)
        vcopy("vector", CTAB[R, :], PI32[R, :], ["PI32"], ["CTAB"])
        ts("vector", CTAB[R, :], CTAB[R, :], smallc[R, 5:6], None, ALU.mult, None, ["CTAB", "invf"], ["CTAB"])

        H = STAB
        H2 = CT2
        ts("vector", H[R, :], CTAB[R, :], 1.0 / TWO_PI, None, ALU.mult, None, ["CTAB"], ["STAB"])
        vcopy("vector", PI32[R, :], H[R, :], ["STAB"], ["PI32"])
        vcopy("vector", H[R, :], PI32[R, :], ["PI32"], ["STAB"])
        stt(H[R, :], H[R, :], -TWO_PI, CTAB[R, :], ALU.mult, ALU.add, ["STAB", "CTAB"], ["STAB"])
        ts("vector", H[R, :], H[R, :], 0.5, None, ALU.mult, None, ["STAB"], ["STAB"])
        tt("vector", H2[R, :], H[R, :], H[R, :], ALU.mult, ["STAB"], ["CT2"])
        SP = PI32[:].bitcast(F32)
        sc = [-1.0 / 39916800.0, 1.0 / 362880.0, -1.0 / 5040.0, 1.0 / 120.0, -1.0 / 6.0, 1.0]
        ts("vector", SP[R, :], H2[R, :], sc[0], None, ALU.mult, None, ["CT2", "PI32"], ["PI32"])
        for cf in sc[1:5]:
            stt(SP[R, :], SP[R, :], cf, H2[R, :], ALU.add, ALU.mult, ["PI32", "CT2"], ["PI32"])
        stt(SP[R, :], SP[R, :], sc[5], H[R, :], ALU.add, ALU.mult, ["PI32", "STAB"], ["PI32"])
        cc = [1.0 / 479001600.0, -1.0 / 3628800.0, 1.0 / 40320.0, -1.0 / 720.0, 1.0 / 24.0, -0.5]
        ts("vector", CTAB[R, :], H2[R, :], cc[0], None, ALU.mult, None, ["CT2", "STAB", "CTAB"], ["CTAB"])
        for cf in cc[1:6]:
            stt(CTAB[R, :], CTAB[R, :], cf, H2[R, :], ALU.add, ALU.mult, ["CTAB", "CT2"], ["CTAB"])
        ts("vector", CTAB[R, :], CTAB[R, :], 1.0, None, ALU.add, None, ["CTAB"], ["CTAB"])
        stt(STAB[R, :], SP[R, :], 2.0, CTAB[R, :], ALU.mult, ALU.mult, ["PI32", "CTAB", "STAB"], ["STAB"])
        tt("vector", CTAB[R, :], SP[R, :], SP[R, :], ALU.mult, ["PI32", "CTAB", "STAB"], ["CTAB"])
        ts("vector", CTAB[R, :], CTAB[R, :], -2.0, 1.0, ALU.mult, ALU.add, ["CTAB"], ["CTAB"])
        ts("vector", STAB[R, :], STAB[R, :], smallc[R, 6:7], None, ALU.mult, None, ["STAB", "sgn"], ["STAB"])

        def make_xt(src_dram):
            for t in range(TT):
                tb = TB[t % 2]
                dma("gpsimd", tb[:], src_dram[t * 128:(t + 1) * 128, :], writes=[f"TB{t % 2}"])
                pb, pbt = nb()
                pv = pb[:].bitcast(BF16).rearrange("p (k t) -> p k t", k=8)

                def fn(e, tb=tb, pv=pv):
                    ins = None
                    for k in range(8):
                        ins = e.transpose(out=pv[:, k, :], in_=tb[:, k * 128:(k + 1) * 128], identity=ident[:])
                    return ins
                S.op("tensor", fn, reads=[f"TB{t % 2}", "ident"], writes=[pbt])
                eng = "vector" if t % 2 == 0 else "scalar"
                if eng == "vector":
                    vcopy("vector", XT[:, :, t * 128:(t + 1) * 128], pv, [pbt], [f"XT{t // 4}"])
                else:
                    S.op("scalar", lambda e, pv=pv, t=t: e.copy(out=XT[:, :, t * 128:(t + 1) * 128], in_=pv), reads=[pbt], writes=[f"XT{t // 4}"])

        final_ops = []
        try:
            make_xt(x)
            ckpt(1, lambda: {"XT": (XT[:, :, 0:512], slice(0, 128), 4096, 0), "CTAB": (CTAB[64:96, 0:512], slice(64, 96), 512, 4096), "STAB": (STAB[64:96, 0:512], slice(64, 96), 512, 4608)})

            def mem_kv(layer):
                wload(WMEM[:], w_mem[layer], ["WMEM"])
                for mt in range(2):
                    tb = TB[0]
                    dma("gpsimd", tb[:], mem[mt * 128:(mt + 1) * 128, :], writes=["TB0"])
                    pb, pbt = nb()
                    pv = pb[:].bitcast(BF16).rearrange("p (k t) -> p k t", k=8)

                    def fn(e, tb=tb, pv=pv):
                        ins = None
                        for k in range(8):
                            ins = e.transpose(out=pv[:, k, :], in_=tb[:, k * 128:(k + 1) * 128], identity=ident[:])
                        return ins
                    S.op("tensor", fn, reads=["TB0", "ident"], writes=[pbt])
                    vcopy("vector", MT[:, :, mt * 128:(mt + 1) * 128], pv, [pbt], ["MT"])
                for ch in range(2):
                    pb, pbt = nb()
                    mm(pb[:, 0:256], [(WMEM[:, k, ch * 128:(ch + 1) * 128], MT[:, k, :]) for k in range(8)], ["WMEM", "MT"], [pbt])
                    vcopy("vector", MK[:, ch, :], pb[:, 0:256], [pbt], ["MK"])
                for mt in range(2):
                    pb, pbt = nb()
                    mm(pb[:, 0:256], [(MT[:, k, mt * 128:(mt + 1) * 128], WMEM[:, k, 256:512]) for k in range(8)], ["WMEM", "MT"], [pbt])
                    vcopy("vector", MVA[:, :, mt, 0:64], pb[:, 0:256].rearrange("p (h d) -> p h d", h=4), [pbt, "MVAones"], ["MVA"])

            def proj_chunk(c, wcols, k_n=8, w=None, rhs=None, rtok=None, wtok="WIN"):
                pb, pbt = nb()
                w = WIN if w is None else w
                rhs = XT if rhs is None else rhs
                rtok = f"XT{c}" if rtok is None else rtok
                mm(pb[0:(wcols.stop - wcols.start), :], [(w[:, k, wcols], rhs[:, k, blk(c)]) for k in range(k_n)], [wtok, rtok], [pbt])
                return pb, pbt

            def gate_qmem(c, col0, w=None, wtok="WIN", wtoks=None):
                for m in range(8):
                    pb, pbt = proj_chunk(c, slice(col0 + m * 128, col0 + (m + 1) * 128), w=w, wtok=(wtoks[m] if wtoks else wtok))
                    act(G[:, m, blk(c)], pb[:], AF.Silu, [pbt], [f"G{m}_{c}"])
                for m in range(2):
                    pb, pbt = proj_chunk(c, slice(col0 + 1024 + m * 128, col0 + 1024 + (m + 1) * 128), w=w, wtok=(wtoks[8 + m] if wtoks else wtok))
                    vcopy("vector", QMB[c % 2][:, m, :], pb[:], [pbt], ["QMB0"])
                ckpt(31, lambda: {"CQT": (CQT[:, :, 0:512], slice(0, 128), 1536, 0), "LATo": (LATo[:, :, 0:512], slice(0, 128), 1024, 1536),
                         "KRo": (KRo[64:96, 0:512], slice(64, 96), 512, 2560), "G": (G[:, :, 0:512], slice(0, 128), 4096, 3072)})
                mem_attention(c)
                ckpt(32, lambda: {"G": (G[:, :, 0:512], slice(0, 128), 4096, 3072)})

            att_state = {"i": 0}
            PTX = list(PT) + [TMP[5][:, 0:256].bitcast(BF16), TMP[5][:, 256:512].bitcast(BF16)]

            def attend(tiles, q_ap, q_tok, scale, o_bank, o_tok, npart, finalize, sb_list=(0, 1, 2), npt=4, LA=2, defer=None):
                n = len(tiles)
                pend = []
                nsb = len(sb_list)

                def issue_qk(i):
                    kT, ktok, v, vtok, mk = tiles[i]
                    gi = att_state["i"]
                    att_state["i"] += 1
                    pb = banks[sb_list[gi % nsb]]
                    pbt = f"bank{sb_list[gi % nsb]}"
                    mm(pb[:], [(kT, q_ap)], (list(ktok) if isinstance(ktok, list) else [ktok]) + [q_tok], [pbt])
                    pt = PTX[gi % npt]
                    ptt = f"PT{gi % npt}"
                    act(pt[:], pb[:], AF.Exp, [pbt], [ptt], scale=scale)
                    if mk is not None:
                        tt("vector", pt[:], pt[:], mk, ALU.mult, [ptt, "MSK"], [ptt])
                    pend.append((i, pt, ptt, v, vtok))

                def issue_pv():
                    i, pt, ptt, v, vtok = pend.pop(0)
                    S.op("tensor", lambda e: e.matmul(o_bank[0:66, :], lhsT=v, rhs=pt[:], start=(i == 0), stop=(i == n - 1)),
                         reads=[ptt, vtok], writes=[o_tok])

                for i in range(n):
                    issue_qk(i)
                    if i >= LA:
                        issue_pv()
                    if defer is not None and i == 8 and defer:
                        defer.pop(0)()
                while pend:
                    issue_pv()
                if defer is None:
                    finalize()
                else:
                    defer.append(finalize)

            fin_rr = [0]

            def finalize_head(o_bank, o_tok, chunk, par, c, bc=(5, 6), rd_alt=False):
                rows = slice(par * 64, par * 64 + 64)
                gt = f"G{chunk}_{c}"
                rdi = 4 + (fin_rr[0] % 2 if rd_alt else 0)
                RDx, rdt = TMP[rdi], f"TMP{rdi}"
                S.op("vector", lambda e: e.reciprocal(out=RDx[64:65, :], in_=o_bank[64:65, :]), reads=[o_tok], writes=[rdt])
                pb, pbt = nb(bc)
                S.op("tensor", lambda e: e.matmul(pb[:], lhsT=ones_f[64:65, :], rhs=RDx[64:65, :], start=True, stop=True), reads=[rdt, "ones_f"], writes=[pbt])
                i = fin_rr[0] % 2
                fin_rr[0] += 1
                t1, t1t = TMP[i], f"TMP{i}"
                t2, t2t = TMP[2 + i], f"TMP{2 + i}"
                S.op("scalar", lambda e: e.copy(out=t1[rows, :], in_=o_bank[0:64, :]), reads=[o_tok], writes=[t1t])
                tt("vector", t2[rows, :], pb[rows, :], G[rows, chunk, blk(c)], ALU.mult, [pbt, gt], [t2t])
                tt("gpsimd", G[rows, chunk, blk(c)], t1[rows, :], t2[rows, :], ALU.mult, [t1t, t2t], [gt])

            def mem_attention(c):
                if True:
                    fins = []
                    for m in range(4):
                        par = m % 2
                        rows = slice(par * 64, par * 64 + 64)
                        ob = banks[(3, 4, 7, 6)[m]]
                        obt = f"bank{(3, 4, 7, 6)[m]}"
                        tiles = [(MK[rows, m // 2, mt * 128:(mt + 1) * 128], "MK", MVA[:, m, mt, :], "MVA", None) for mt in range(2)]
                        attend(tiles, QMB[c % 2][rows, m // 2, :], "QMB0", SC_MEM, ob, obt, 64,
                               lambda ob=ob, obt=obt, m=m, par=par, c=c: finalize_head(ob, obt, 6 + m // 2, par, c, bc=(5, 0), rd_alt=True),
                               defer=fins)
                    while fins:
                        fins.pop(0)()

            def out_ln(layer, res_src, dst_dram, also_xt, w_prefetched=False):
                if not w_prefetched:
                    wload(WSM[:], w_out[layer], ["WSM", "WSMs"])
                dma("sync", LG[:], ln_g[layer:layer + 1, :].partition_broadcast(128), writes=["LG"])
                dma("sync", LB[:], ln_b[layer:layer + 1, :].partition_broadcast(128), writes=["LB"])
                def load_res(t):
                    dma("sync", RB[t % 6][:], res_src[t * 128:(t + 1) * 128, :], writes=[f"RB{t % 6}"])
                for t in range(6):
                    load_res(t)
                def s1(t):
                    rb, rbt = RB[t % 6], f"RB{t % 6}"
                    stat, stt_ = STATS[t % 3], f"stat{t % 3}"
                    c = t // 4
                    for n_ in range(2):
                        pb, pbt = nb()
                        mm(pb[:], [(G[:, k, t * 128:(t + 1) * 128], WSM[:, k, n_ * 512:(n_ + 1) * 512]) for k in range(8)],
                           [f"G{m}_{c}" for m in range(8)] + ["WSM"], [pbt])
                        stt(rb[:, n_ * 512:(n_ + 1) * 512], rb[:, n_ * 512:(n_ + 1) * 512], ALPHA, pb[:], ALU.mult, ALU.add, [pbt, rbt], [rbt])
                        S.op("vector", lambda e, rb=rb, n_=n_, stat=stat: e.bn_stats(out=stat[:, n_ * 6:(n_ + 1) * 6], in_=rb[:, n_ * 512:(n_ + 1) * 512]), reads=[rbt], writes=[stt_])
                    S.op("vector", lambda e, stat=stat: e.bn_aggr(out=stat[:, 12:14], in_=stat[:, 0:12]), reads=[stt_], writes=[stt_])
                    ts("vector", stat[:, 14:15], stat[:, 13:14], EPS, None, ALU.add, None, [stt_], [stt_])
                    act(stat[:, 15:16], stat[:, 14:15], AF.Ln, [stt_], [stt_])
                    act(stat[:, 16:17], stat[:, 15:16], AF.Exp, [stt_], [stt_], scale=-0.5)

                def s3(t):
                    rb, rbt = RB[t % 6], f"RB{t % 6}"
                    stat, stt_ = STATS[t % 3], f"stat{t % 3}"
                    ts("vector", rb[:], rb[:], stat[:, 12:13], stat[:, 16:17], ALU.subtract, ALU.mult, [rbt, stt_], [rbt])
                    tt("gpsimd", rb[:], rb[:], LG[:], ALU.mult, [rbt, "LG"], [rbt])
                    tt("gpsimd", rb[:], rb[:], LB[:], ALU.add, [rbt, "LB"], [rbt])
                    o = dma("sync", dst_dram[t * 128:(t + 1) * 128, :], rb[:], reads=[rbt], writes=[f"dst{t}"])
                    final_ops.append(o)
                    if also_xt:
                        tb = TB[t % 2]
                        tbt = f"TB{t % 2}"
                        S.op("scalar", lambda e, tb=tb, rb=rb: e.copy(out=tb[:], in_=rb[:]), reads=[rbt], writes=[tbt])
                    if t + 6 < TT:
                        load_res(t + 6)

                def s3b(t):
                    if also_xt:
                        tb = TB[t % 2]
                        tbt = f"TB{t % 2}"
                        pb, pbt = nb()
                        pv = pb[:].bitcast(BF16).rearrange("p (k t) -> p k t", k=8)

                        def fn(e, tb=tb, pv=pv):
                            ins = None
                            for k in range(8):
                                ins = e.transpose(out=pv[:, k, :], in_=tb[:, k * 128:(k + 1) * 128], identity=ident[:])
                            return ins
                        S.op("tensor", fn, reads=[tbt, "ident"], writes=[pbt])
                        S.op("scalar", lambda e, pv=pv, t=t: e.copy(out=XT[:, :, t * 128:(t + 1) * 128], in_=pv), reads=[pbt], writes=[f"XT{t // 4}"])

                for t in range(TT + 3):
                    if t < TT:
                        s1(t)
                    if 0 <= t - 2 < TT:
                        s3(t - 2)
                    if 0 <= t - 3 < TT:
                        s3b(t - 3)


            wload(WIN[:, :, 0:672], w_in0[:, 0:672], ["WIN"])
            wload(WIN[:, :, 672:1312], w_in0[:, 672:1312], ["WINb"])
            wload(WIN[:, :, 1312:1952], w_in0[:, 1312:1952], ["WINc"])
            dma("gpsimd", WKS[:, :, 64:96], w_krs.rearrange("(k p) c -> p k c", p=128), writes=["WKS"])
            WFLAT = WSM[:].rearrange("p k c -> p (k c)")
            WUQ = WFLAT[:, 0:3456].rearrange("p (k c) -> p k c", k=3)
            WUQS = WFLAT[:, 3456:6912].rearrange("p (k c) -> p k c", k=3)
            wload(WUQ, w_uq, ["WSM"])
            wload(WUQS, w_uqs, ["WSMs"])
            mem_kv(0)
            ckpt(2, lambda: {"MK": (MK[:], slice(0, 128), 512, 0), "MVA": (MVA[:, :, 0, :], slice(0, 128), 264, 512)})
            S.barrier(lambda e: e.memset(stat[:, 19:20], 0.0), skip_dma=True)
            for c in range(NB):
                for (nch, col0, ncols, gcol, dst, dtok) in ((3, 0, 384, 0, CQT, "CQT"), (2, 384, 256, 3, LATo, "LATo")):
                    c32 = []
                    sqs = []
                    for i in range(nch):
                        pb, pbt = proj_chunk(c, slice(col0 + i * 128, col0 + (i + 1) * 128))
                        sq = PT[i]
                        if KF >= 2:
                            act(sq[:], pb[:], AF.Square, [pbt], [f"PT{i}"])
                        vcopy("vector", TMP[2 + i][:], pb[:], [pbt], [f"TMP{2 + i}"])
                        sqs.append((sq, f"PT{i}"))
                        c32.append((TMP[2 + i], f"TMP{2 + i}"))
                    pb, pbt = nb()
                    if KF >= 3:
                        mm(pb[:], [(ones_bf[:], sq[:]) for sq, _ in sqs], [t for _, t in sqs] + ["ones_bf"], [pbt])
                        ts("vector", TMP[5][:], pb[:], 1.0 / ncols, EPS, ALU.mult, ALU.add, [pbt], ["TMP5"])
                    if KF >= 4:
                        act(TMP[5][:], TMP[5][:], AF.Ln, ["TMP5"], ["TMP5"])
                        act(TMP[5][:], TMP[5][:], AF.Exp, ["TMP5"], ["TMP5"], scale=-0.5)
                    for i in range(nch if KF >= 5 else 0):
                        t32, t32t = c32[i]
                        stt(dst[:, i, blk(c)], t32[:], smallc[:, gcol + i:gcol + i + 1], TMP[5][:], ALU.mult, ALU.mult,
                            [t32t, "TMP5", "qn", "kvn"], [f"{dtok}{c}"])
                    ckpt(311 if dtok == "CQT" else 312, lambda: {"CQT": (CQT[:, :, 0:512], slice(0, 128), 1536, 0), "LATo": (LATo[:, :, 0:512], slice(0, 128), 1024, 1536)})
                pb, pbt = proj_chunk(c, slice(576, 672))
                pb2, pbt2 = proj_chunk(c, slice(0, 96), w=WKS, wtok="WKS")
                tt("vector", TMP[0][R, :], pb[R, :], CTAB[R, blk(c)], ALU.mult, [pbt, "CTAB"], ["TMP0"])
                tt("vector", TMP[1][R, :], pb2[R, :], STAB[R, blk(c)], ALU.mult, [pbt2, "STAB"], ["TMP1"])
                tt("vector", KRo[R, blk(c)], TMP[0][R, :], TMP[1][R, :], ALU.add, ["TMP0", "TMP1"], ["KRo"])
                ckpt(313, lambda: {"CQT": (CQT[:, :, 0:512], slice(0, 128), 1536, 0), "LATo": (LATo[:, :, 0:512], slice(0, 128), 1024, 1536), "KRo": (KRo[64:96, 0:512], slice(64, 96), 512, 2560)})

            ckpt(3, lambda: {"CQT": (CQT[:, :, 0:512], slice(0, 128), 1536, 0), "LATo": (LATo[:, :, 0:512], slice(0, 128), 1024, 1536),
                     "KRo": (KRo[64:96, 0:512], slice(64, 96), 512, 2560), "G": (G[:, :, 0:512], slice(0, 128), 4096, 3072)})
            for k in range(2):
                dma("sync", lat_b[k].ap(), LATo[:, k, :], reads=[f"LATo{c}" for c in range(NB)], writes=[f"lat_b{k}"])
            dma("sync", lat_b[2].ap(), KRo[R, :], reads=["KRo"], writes=["lat_b2"])
            for i in range(3):
                S.op("gpsimd", lambda e, i=i: e.collective_compute("AllGather", ALU.bypass, replica_groups=groups, ins=[lat_b[i].ap().opt()], outs=[lat_g[i].ap().opt()]),
                     reads=[f"lat_b{i}"], writes=[f"lat_g{i}"], dma="cc")
            for c in range(NB):
                gate_qmem(c, 672, wtoks=["WINb"] * 5 + ["WINc"] * 5)

            ckpt(4)
            S.barrier(lambda e: e.memset(stat[:, 20:21], 0.0), skip_cc=True)
            S.op("gpsimd", lambda e: e.memset(stat[:, 26:27], 0.0), writes=["QTGATE"])
            for jp in range(4):
                for k in range(2):
                    dma("sync", LAT[:, k, :].rearrange("p (c j t) -> p c j t", c=4, j=4)[:, :, jp, :],
                        lat_g[k].ap()[jp * 128:(jp + 1) * 128, :].rearrange("p (c t) -> p c t", c=4),
                        reads=[f"lat_g{k}"], writes=["LAT"])
            qit = 0
            for c in range(NB):
                for h in range(12):
                    hc = slice(h * 96, (h + 1) * 96)
                    ta, tat = TMP[2 * (qit % 2)], f"TMP{2 * (qit % 2)}"
                    tb_, tbt_ = TMP[2 * (qit % 2) + 1], f"TMP{2 * (qit % 2) + 1}"
                    qit += 1
                    pb, pbt = nb()
                    mm(pb[0:96, :], [(WUQ[:, k, hc], CQT[:, k, blk(c)]) for k in range(3)], ["WSM", f"CQT{c}"], [pbt])
                    pb2, pbt2 = nb()
                    mm(pb2[0:96, :], [(WUQS[:, k, hc], CQT[:, k, blk(c)]) for k in range(3)], ["WSMs", f"CQT{c}"], [pbt2])
                    S.op("scalar", lambda e, pb=pb, h=h, c=c: e.copy(out=QT[0:64, h, blk(c)], in_=pb[0:64, :]), reads=[pbt, "QTGATE"], writes=[f"QT{c}"], merge=True)
                    tt("vector", ta[R, :], pb[R, :], CTAB[R, blk(c)], ALU.mult, [pbt, "CTAB"], [tat])
                    tt("vector", tb_[R, :], pb2[R, :], STAB[R, blk(c)], ALU.mult, [pbt2, "STAB"], [tbt_])
                    S.op("gpsimd", lambda e, h=h, c=c, ta=ta, tb_=tb_: e.tensor_tensor(out=QT[R, h, blk(c)], in0=ta[R, :], in1=tb_[R, :], op=ALU.add),
                         reads=[tat, tbt_, "QTGATE"], writes=[f"QT{c}"], merge=True)


            ckpt(5, lambda: {"QT": (QT[0:96, 0:4, 0:512], slice(0, 96), 2048, 0), "G": (G[:, :, 0:512], slice(0, 128), 4096, 3072)})
            S.barrier(lambda e: e.memset(stat[:, 21:22], 0.0))
            for i in range(2):
                S.op("gpsimd", lambda e, i=i: e.memset(VA[i][:, :, 64:66], 1.0), writes=[f"VA{i}"])
            for jp in range(4):
                for i in range(1):
                    dma("sync", KT[i][64:96, :].rearrange("p (c j t) -> p c j t", c=4, j=4)[:, :, jp, :],
                        lat_g[2].ap()[jp * 32:(jp + 1) * 32, :].rearrange("p (c t) -> p c t", c=4),
                        reads=["lat_g2"], writes=[f"KTr{i}"])
            WUKV = WSM[:].rearrange("p k c -> p (k c)")[:, 0:3072].rearrange("p (k c) -> p k c", k=2)
            wload(WUKV, w_ukv, ["WSM", "WSMs"])

            def wukv(k, col):
                return WUKV[:, k, col], "WSM"

            ckpt(6, lambda: {"LAT": (LAT[:, 0, 0:4096], slice(0, 128), 4096, 0), "KT": (KT[0][64:96, 0:4096], slice(64, 96), 4096, 4096)})
            deferred_fin = []
            for h in range(NHEADS_DBG):
                b_ = h % 2
                for kb in range(16):
                    pb, pbt = nb((5, 6))
                    wc = slice(h * 128, h * 128 + 64)
                    mm(pb[0:64, :], [(wukv(k, wc)[0], LAT[:, k, blk(kb)]) for k in range(2)], [wukv(0, wc)[1], "LAT"], [pbt])
                    if kb % 2 == 0:
                        vcopy("vector", KT[0][0:64, blk(kb)], pb[0:64, :], [pbt], [f"KT0_{kb}"])
                    else:
                        S.op("scalar", lambda e, pb=pb, b_=b_, kb=kb: e.copy(out=KT[0][0:64, blk(kb)], in_=pb[0:64, :]), reads=[pbt], writes=[f"KT0_{kb}"])
                for g8 in range(8):
                    pb, pbt = nb((5, 6))
                    wc = slice(h * 128 + 64, h * 128 + 128)

                    def fn(e, pb=pb, g8=g8, wc=wc):
                        ins = None
                        for i in range(8):
                            kt = g8 * 8 + i
                            for k in range(2):
                                ins = e.matmul(pb[:, i * 64:(i + 1) * 64], lhsT=LAT[:, k, kt * 128:(kt + 1) * 128], rhs=wukv(k, wc)[0], start=(k == 0), stop=(k == 1))
                        return ins
                    S.op("tensor", fn, reads=["LAT", wukv(0, wc)[1]], writes=[pbt])
                    vcopy("vector", VA[b_][:, g8 * 8:(g8 + 1) * 8, 0:64], pb[:].rearrange("p (t d) -> p t d", t=8), [pbt, f"VA{b_}"], [f"VA{b_}_{g8}"])
                if h == NHEADS_DBG - 1:
                    wload(WSM[:], w_out[0], ["WSM", "WSMs"])
                for c in range(NB):
                    nkb = 4 * c + 4
                    tiles = []
                    for kb in range(nkb):
                        for q4 in range(4):
                            kt = kb * 4 + q4
                            mk = MSK[:, (kb - 4 * c) * 4 + q4, :] if kb >= 4 * c else None
                            tiles.append((KT[0][0:96, kt * 128:(kt + 1) * 128], [f"KT0_{kb}", "KTr0"], VA[b_][:, kt, :], f"VA{b_}_{kt // 8}", mk))
                    ob = banks[3 + (h * NB + c) % 2]
                    obt = f"bank{3 + (h * NB + c) % 2}"
                    attend(tiles, QT[0:96, h, blk(c)], f"QT{c}", SC_ATT, ob, obt, 96,
                           lambda ob=ob, obt=obt, h=h, c=c: finalize_head(ob, obt, h // 2, h % 2, c),
                           sb_list=(0, 1, 2, 7), npt=6, LA=3, defer=deferred_fin)
            while deferred_fin:
                deferred_fin.pop(0)()

            ckpt(7, lambda: {"G": (G[:, :, 0:512], slice(0, 128), 4096, 0), "G3": (G[:, :, 1536:2048], slice(0, 128), 4096, 4096)})
            S.barrier(lambda e: e.memset(stat[:, 23:24], 0.0))
            out_ln(0, x, hs.ap(), True, w_prefetched=True)
            S.barrier(lambda e: e.memset(stat[:, 24:25], 0.0))
            ckpt(8)
            final_ops.clear()

            WU1 = WSM[:].rearrange("p k c -> p (k c)")[:, 0:6144].rearrange("p (k c) -> p k c", k=8)
            wload(WU1, w_in1[:, 0:768], ["WSM", "WSMs"])
            dma("sync", L1C[:, 0:48], l1pack[:, :], writes=["L1C"])
            dma("gpsimd", WR[:], w_rbd[:, :, :], writes=["WR"])
            dma("gpsimd", WI[:], w_ibd[:, :, :], writes=["WI"])
            act(L1C[:, 48:54], L1C[:, 42:48], AF.Exp, ["L1C"], ["L1C"], scale=-1.0)
            act(L1C[:, 48:54], L1C[:, 48:54], AF.Ln, ["L1C"], ["L1C"], bias=1.0, scale=1.0)
            ts("vector", L1C[:, 54:60], L1C[:, 48:54], -16.0, None, ALU.mult, None, ["L1C"], ["L1C"])
            ts("vector", L1C[:, 48:54], L1C[:, 48:54], -8.0, None, ALU.mult, None, ["L1C"], ["L1C"])

            UT = sb("UT", [128, 6, 16], F32)
            S.op("gpsimd", lambda e: e.memset(UT[:], 0.0), writes=["UT"])
            for c in range(NB):
                for ch in range(6):
                    pb, pbt = nb()
                    mm(pb[:, 0:3], [(WU1[:, k, ch * 128:(ch + 1) * 128], XT[:, k, c * 512 + 509:c * 512 + 512]) for k in range(8)], ["WSM", f"XT{c}"], [pbt])
                    vcopy("vector", UT[:, ch, c * 4:c * 4 + 3], pb[:, 0:3], [pbt], ["UT"])
            dma("sync", uh_b.ap().rearrange("(k p) t -> p k t", p=128), UT[:], reads=["UT"], writes=["uh_b"])
            S.op("gpsimd", lambda e: e.collective_compute("AllGather", ALU.bypass, replica_groups=groups, ins=[uh_b.ap().opt()], outs=[uh_g.ap().opt()]),
                 reads=["uh_b"], writes=["uh_g"], dma="cc")
            mem_kv(1)
            wload(WG1, w_in1[:, 768:2048], ["WIN", "WMEM", "MT", "TB0", "TB1"])
            HALL = sb("HALL", [128, 6, 4, 16], F32)
            for jp in range(4):
                dma("sync", HALL[:, :, jp, :], uh_g.ap()[jp * 768:(jp + 1) * 768, :].rearrange("(k p) t -> p k t", p=128), reads=["uh_g"], writes=["HALL"])
            S.op("gpsimd", lambda e: e.memset(HALO[:], 0.0), writes=["HALO"])
            for c in range(NB):
                for jj in range(4):
                    if jj == 0 and c == 0:
                        continue
                    src = HALL[:, :, jj - 1, c * 4:c * 4 + 3] if jj >= 1 else HALL[:, :, 3, (c - 1) * 4:(c - 1) * 4 + 3]
                    stt(HALO[:, :, c * 4:c * 4 + 3], src, SEL[:, jj:jj + 1], HALO[:, :, c * 4:c * 4 + 3], ALU.mult, ALU.add, ["HALL", "SEL", "HALO"], ["HALO"])

            ckpt(9)
            TMPB = [view(A4, 10240 + i * 2048, 2048, F32) for i in range(4)] + \
                   [MSK[:].rearrange("p a b -> p (a b)")[:, i * 2048:(i + 1) * 2048].bitcast(F32) for i in range(4)]
            ZER = TMPB[6]
            S.op("gpsimd", lambda e: e.memset(ZER[:], 0.0), writes=["ZER"])
            def ctx(it):
                c, ch = it // 6, it % 6

                def T(k):
                    return (TMP[k], f"TMP{k}") if it % 2 == 0 else (TMPB[k], f"TMPB{k}")
                return c, ch, T, (UX[it % 2], f"UX{it % 2}"), (XCB[it % 2], f"XCB{it % 2}")

            gate_ps = {}

            def stA(it):
                c, ch, T, (ux, uxt), (xcb, xcbt) = ctx(it)
                pb, pbt = proj_chunk(c, slice(ch * 128, (ch + 1) * 128), w=WU1, wtok="WSM")
                S.op("scalar", lambda e, ux=ux, pb=pb: e.copy(out=ux[:, 3:515], in_=pb[:]), reads=[pbt], writes=[uxt])
                vcopy("vector", ux[:, 0:3], HALO[:, ch, c * 4:c * 4 + 3], ["HALO", uxt], [uxt])
                xc, xct = T(0)
                cw = lambda tap: L1C[:, ch * 4 + tap:ch * 4 + tap + 1]
                ts("vector", xc[:], ux[:, 0:512], cw(0), L1C[:, 24 + ch:25 + ch], ALU.mult, ALU.add, [uxt, "L1C"], [xct])
                for tap in range(1, 4):
                    stt(xc[:], ux[:, tap:tap + 512], cw(tap), xc[:], ALU.mult, ALU.add, [uxt, "L1C", xct], [xct])
                vcopy("vector", xcb[:], xc[:], [xct], [xcbt])
                pr, prt = nb()
                mm(pr[:], [(WR[:, ch, :], xcb[:])], ["WR", xcbt], [prt])
                pi_, pit = nb()
                mm(pi_[:], [(WI[:, ch, :], xcb[:])], ["WI", xcbt], [pit])
                gate_ps[it] = (pr, prt, pi_, pit)

            def stB(it):
                c, ch, T, _, _ = ctx(it)
                pr, prt, pi_, pit = gate_ps[it]
                r_, rt = T(1)
                i_, itk = T(2)
                a_, at_ = T(3)
                m_, mt_ = T(4)
                act(r_[:], pr[:], AF.Sigmoid, [prt, "L1C"], [rt], bias=L1C[:, 30 + ch:31 + ch])
                act(i_[:], pi_[:], AF.Sigmoid, [pit, "L1C"], [itk], bias=L1C[:, 36 + ch:37 + ch])
                act(a_[:], r_[:], AF.Exp, [rt, "L1C"], [at_], scale=L1C[:, 48 + ch:49 + ch])
                act(m_[:], r_[:], AF.Exp, [rt, "L1C"], [mt_], scale=L1C[:, 54 + ch:55 + ch])
                act(m_[:], m_[:], AF.Sqrt, [mt_], [mt_], scale=-1.0, bias=1.0)

            def stC(it):
                c, ch, T, _, _ = ctx(it)
                xc, xct = T(0)
                i_, itk = T(2)
                a_, at_ = T(3)
                m_, mt_ = T(4)
                tt("gpsimd", i_[:], i_[:], xc[:], ALU.mult, [itk, xct], [itk])
                tt("gpsimd", i_[:], i_[:], m_[:], ALU.mult, [itk, mt_], [itk])
                hl, hlt = T(5)
                pp, ppt = T(1)
                S.op("vector", lambda e, hl=hl, a_=a_, i_=i_: e.tensor_tensor_scan(out=hl[:], data0=a_[:], data1=i_[:], initial=0.0, op0=ALU.mult, op1=ALU.add), reads=[at_, itk], writes=[hlt])
                S.op("vector", lambda e, pp=pp, a_=a_: e.tensor_tensor_scan(out=pp[:], data0=a_[:], data1=ZER[:], initial=1.0, op0=ALU.mult, op1=ALU.add), reads=[at_, "ZER"], writes=[ppt])
                vcopy("gpsimd", HL[:, ch, blk(c)], hl[:], [hlt], [f"HL{c}"])
                vcopy("gpsimd", PP[:, ch, blk(c)], pp[:], [ppt], [f"PP{c}"])
                vcopy("vector", SUMS[:, ch, c * 2:c * 2 + 1], pp[:, 511:512], [ppt], ["SUMS"])
                vcopy("vector", SUMS[:, ch, c * 2 + 1:c * 2 + 2], hl[:, 511:512], [hlt], ["SUMS"])

            NIT = NB * 6
            stA(0)
            for it in range(NIT):
                if it + 1 < NIT:
                    stA(it + 1)
                stB(it)
                stC(it)
            dma("sync", sm_b.ap().rearrange("(k p) t -> p k t", p=128), SUMS[:, :, 0:8], reads=["SUMS"], writes=["sm_b"])
            S.op("gpsimd", lambda e: e.collective_compute("AllGather", ALU.bypass, replica_groups=groups, ins=[sm_b.ap().opt()], outs=[sm_g.ap().opt()]),
                 reads=["sm_b"], writes=["sm_g"], dma="cc")
            ckpt(10)
            wload(WSM[:], w_out[1], ["WSM", "WSMs"])
            SALL = sb("SALL", [128, 6, 4, 8], F32)
            STT = sb("STT", [128, 6, 17], F32)
            HIN = sb("HIN", [128, 6, 4], F32)

            def carry_states():
                for jp in range(4):
                    dma("sync", SALL[:, :, jp, :], sm_g.ap()[jp * 768:(jp + 1) * 768, :].rearrange("(k p) t -> p k t", p=128), reads=["sm_g"], writes=["SALL"])
                S.op("gpsimd", lambda e: e.memset(STT[:], 0.0), writes=["STT"])
                for g in range(16):
                    cc_, jj = g // 4, g % 4
                    tt("vector", STT[:, :, g + 1:g + 2], SALL[:, :, jj, cc_ * 2:cc_ * 2 + 1], STT[:, :, g:g + 1], ALU.mult, ["SALL", "STT"], ["STT"])
                    tt("vector", STT[:, :, g + 1:g + 2], STT[:, :, g + 1:g + 2], SALL[:, :, jj, cc_ * 2 + 1:cc_ * 2 + 2], ALU.add, ["SALL", "STT"], ["STT"])
                S.op("gpsimd", lambda e: e.memset(HIN[:], 0.0), writes=["HIN"])
                for c in range(NB):
                    for jj in range(4):
                        stt(HIN[:, :, c:c + 1], STT[:, :, 4 * c + jj:4 * c + jj + 1], SEL[:, jj:jj + 1], HIN[:, :, c:c + 1], ALU.mult, ALU.add, ["STT", "SEL", "HIN"], ["HIN"])

            def fixup(c):
                for ch in range(6):
                    t0, t0t = TMP[(c * 6 + ch) % 2], f"TMP{(c * 6 + ch) % 2}"
                    stt(t0[:], PP[:, ch, blk(c)], HIN[:, ch, c:c + 1], HL[:, ch, blk(c)], ALU.mult, ALU.add, [f"PP{c}", f"HL{c}", "HIN"], [t0t])
                    tt("gpsimd", G[:, ch, blk(c)], t0[:], G[:, ch, blk(c)], ALU.mult, [t0t, f"G{ch}_{c}"], [f"G{ch}_{c}"])

            for c in range(NB):
                gate_qmem(c, 0, w=WG1, wtok="WIN")
                if c == 0:
                    carry_states()
                fixup(c)
            ckpt(11)
            S.barrier(lambda e: e.memset(stat[:, 25:26], 0.0))
            out_ln(1, hs.ap(), out, False, w_prefetched=True)
        except _Stop:
            pass
        if not final_ops:
            for name, (ap, rows, ncols, off) in (DUMPS.items() if not os.environ.get('KNODUMP') else []):
                dst = dbgout[rows, off:off + ncols]
                if len(ap.shape) == 3:
                    dst = dst.rearrange("p (k t) -> p k t", k=ap.shape[1])
                final_ops.append(dma("gpsimd", dst, ap, reads=list(S.last_w.keys())))
            final_ops.append(dma("sync", out[0:128, :], x[0:128, :]))
        S.emit(final_wait_ops=list(final_ops))
    return nc


_NC_CACHE = {}


def _prep_inputs(inp):
    f32 = np.float32
    x = np.asarray(inp["x"], f32)
    mem = np.asarray(inp["mem"], f32)
    pos = np.asarray(inp["positions"], np.int32)
    w_in0 = np.ascontiguousarray(np.asarray(inp["mla_w_in"], f32)[0])
    kr = w_in0[:, 640:672]
    w_krs = np.ascontiguousarray(np.concatenate([kr[:, 16:32], kr[:, 0:16]], axis=1))
    w_uq = np.ascontiguousarray(np.asarray(inp["mla_w_uq"], f32)[0])
    wq3 = w_uq.reshape(384, 12, 96)
    w_uqs = np.ascontiguousarray(np.concatenate([wq3[:, :, 0:64], wq3[:, :, 80:96], wq3[:, :, 64:80]], axis=2).reshape(384, 1152))
    qn = np.ascontiguousarray(np.asarray(inp["mla_q_norm"], f32)[0].reshape(3, 128).T)
    kvn = np.ascontiguousarray(np.asarray(inp["mla_kv_norm"], f32)[0].reshape(2, 128).T)
    w_ukv = np.ascontiguousarray(np.asarray(inp["mla_w_ukv"], f32)[0])
    w_in1 = np.ascontiguousarray(np.asarray(inp["lru_w_in"], f32)[0])
    cw = np.asarray(inp["lru_conv_w"], f32)[0]
    conv_w = np.ascontiguousarray(cw.reshape(4, 6, 128).transpose(2, 1, 0))
    pp = lambda v: np.ascontiguousarray(np.asarray(v, f32)[0].reshape(6, 128).T)
    conv_b = pp(inp["lru_conv_b"])
    b_r = pp(inp["lru_b_rgate"])
    b_i = pp(inp["lru_b_igate"])
    lam = pp(inp["lru_lambda"])

    def blockdiag(w):
        w = np.asarray(w, f32)[0]
        o = np.zeros((128, 6, 128), f32)
        for g in range(12):
            ch, e = g // 2, g % 2
            o[e * 64:(e + 1) * 64, ch, e * 64:(e + 1) * 64] = w[g]
        return o
    w_rbd = blockdiag(inp["lru_w_rgate"])
    w_ibd = blockdiag(inp["lru_w_igate"])
    w_mem = np.ascontiguousarray(np.asarray(inp["w_mem_kv"], f32))
    w_out = np.ascontiguousarray(np.asarray(inp["w_out"], f32))
    ln_g = np.ascontiguousarray(np.asarray(inp["ln_g"], f32))
    ln_b = np.ascontiguousarray(np.asarray(inp["ln_b"], f32))
    invf16 = (np.float32(10000.0) ** (-np.arange(16, dtype=np.float32) / np.float32(16))).astype(f32)
    invf = np.concatenate([invf16, invf16]).reshape(32, 1).astype(f32)
    sgn = np.concatenate([-np.ones(16, f32), np.ones(16, f32)]).reshape(32, 1)
    l1pack = np.ascontiguousarray(np.concatenate([conv_w.reshape(128, 24), conv_b, b_r, b_i, lam], axis=1)).astype(f32)
    shared = dict(w_in0=w_in0, w_krs=w_krs, w_uq=w_uq, w_uqs=w_uqs, w_ukv=w_ukv, w_in1=w_in1,
                  w_rbd=w_rbd, w_ibd=w_ibd, l1pack=l1pack,
                  w_mem=w_mem, w_out=w_out, ln_g=ln_g, ln_b=ln_b)
    maps = []
    kk = np.arange(128)[:, None]
    qq = np.arange(512)[None, :]
    for r in range(8):
        b, j = r // 4, r % 4
        tok = np.concatenate([np.arange((4 * c + j) * 512, (4 * c + j + 1) * 512) for c in range(4)])
        m = dict(shared)
        m["x"] = np.ascontiguousarray(x[b, tok])
        m["posb"] = np.ascontiguousarray(np.broadcast_to(pos[b, tok][None, :], (32, NT))).astype(np.int32)
        m["mem"] = np.ascontiguousarray(mem[b])
        msk = np.zeros((128, 16, 512), np.uint8)
        for g in range(4):
            for q4 in range(4):
                if g < j:
                    msk[:, g * 4 + q4, :] = 1
                elif g == j:
                    msk[:, g * 4 + q4, :] = ((q4 * 128 + kk) <= qq).astype(np.uint8)
        m["masks"] = msk
        cp = np.zeros((128, 64), f32)
        cp[:, 0:3] = qn
        cp[:, 3:5] = kvn
        cp[64:96, 5:6] = invf
        cp[64:96, 6:7] = sgn
        cp[:, 8 + j] = 1.0
        m["cpack"] = cp
        maps.append(m)
    return maps


def kernel(**inputs):
    if "nc" not in _NC_CACHE:
        _NC_CACHE["nc"] = build()
    nc = _NC_CACHE["nc"]
    maps = _prep_inputs(inputs)
    res = run_bass_kernel_spmd(nc, maps, core_ids=list(range(8)))
    outp = np.zeros((2, SEQ, 1024), np.float32)
    for r in range(8):
        b, j = r // 4, r % 4
        o = np.asarray(res.results[r]["out"], np.float32)
        for c in range(4):
            outp[b, (4 * c + j) * 512:(4 * c + j + 1) * 512] = o[c * 512:(c + 1) * 512]
    return outp
```
